# Optimizing a Trainium2 kernel written in Bass

```python
import math
import jax, jax.numpy as jnp
from jax import lax
import numpy as np

D_MODEL = 1024
BATCH = 16
SEQ = 2048
DEPTH = 4

N_MIXERS = 2
N_A_LAYERS = (DEPTH + 1) // 2
N_B_LAYERS = DEPTH // 2

GLA_HEADS = 4
GLA_DK = (D_MODEL // 2) // GLA_HEADS
GLA_DV = D_MODEL // GLA_HEADS
GLA_GATE_RANK = 16
GLA_GATE_NORM = 16.0
GLA_CHUNK = 64
GLA_QK = GLA_HEADS * GLA_DK
GLA_V = GLA_HEADS * GLA_DV
GLA_IN = 2 * GLA_QK + 2 * GLA_V + 2 * GLA_GATE_RANK

DIFF_HEADS = 8
DIFF_DH = D_MODEL // DIFF_HEADS // 2
DIFF_QK = DIFF_HEADS * 2 * DIFF_DH
DIFF_V = DIFF_HEADS * 2 * DIFF_DH
DIFF_IN = 2 * DIFF_QK + DIFF_V
Q_BLOCK = 128

REL_BUCKETS = 32
REL_MAX_DIST = 128

D_FF = 2816
EPS = 1e-6

kernel_name = "hybrid_gla_diffattn_macaron_encoder"


def rms_norm(x, g):
    xf = x.astype(jnp.float32)
    y = xf * lax.rsqrt(jnp.mean(xf * xf, axis=-1, keepdims=True) + EPS)
    return (y * g.astype(jnp.float32)).astype(x.dtype)


def swiglu_ffn(h, w_gu, w_down):
    gate, up = jnp.split(h @ w_gu, 2, axis=-1)
    return (jax.nn.silu(gate) * up) @ w_down


def gla_chunked(q, k, v, log_a, strict):
    B, H, S, dk = q.shape
    dv = v.shape[-1]
    n = S // GLA_CHUNK

    def to_chunks(t):
        return jnp.moveaxis(t.reshape(B, H, n, GLA_CHUNK, t.shape[-1]), 2, 0)

    mask = jnp.tril(jnp.ones((GLA_CHUNK, GLA_CHUNK), dtype=bool), k=-1 if strict else 0)

    def step(state, inp):
        qc, kc, vc, gc = inp
        b = jnp.cumsum(gc, axis=-2)
        inter = jnp.einsum('bhcd,bhdv->bhcv', qc * jnp.exp(b), state)
        rel = jnp.minimum(b[:, :, :, None, :] - b[:, :, None, :, :], 0.0)
        decay = jnp.where(mask[:, :, None], jnp.exp(rel), 0.0)
        scores = jnp.sum(qc[:, :, :, None, :] * kc[:, :, None, :, :] * decay, axis=-1)
        intra = jnp.einsum('bhij,bhjv->bhiv', scores, vc)
        b_end = b[:, :, -1:, :]
        k_dec = kc * jnp.exp(b_end - b)
        new_state = jnp.exp(b_end[:, :, 0, :])[..., None] * state + jnp.einsum('bhcd,bhcv->bhdv', k_dec, vc)
        return new_state, inter + intra

    state0 = jnp.zeros((B, H, dk, dv), jnp.float32)
    _, out = lax.scan(step, state0, (to_chunks(q), to_chunks(k), to_chunks(v), to_chunks(log_a)))
    return jnp.moveaxis(out, 0, 2).reshape(B, H, S, dv)


def gla_mixer(h, w_in, w_gate2, b_gate, o_norm_g, w_out):
    B, S, _ = h.shape
    proj = h @ w_in
    q, k, v, g, lr = jnp.split(proj, [GLA_QK, 2 * GLA_QK, 2 * GLA_QK + GLA_V, 2 * GLA_QK + 2 * GLA_V], axis=-1)
    lr_f, lr_b = lr[..., :GLA_GATE_RANK], lr[..., GLA_GATE_RANK:]

    def heads(t, d):
        return t.reshape(B, S, GLA_HEADS, d).transpose(0, 2, 1, 3).astype(jnp.float32)

    def log_decay(lr_d, w2, bias):
        z = (lr_d @ w2 + bias).astype(jnp.float32)
        return jax.nn.log_sigmoid(z) / GLA_GATE_NORM

    qh = heads(q, GLA_DK) * (GLA_DK ** -0.5)
    kh = heads(k, GLA_DK)
    vh = heads(v, GLA_DV)
    la_f = heads(log_decay(lr_f, w_gate2[0], b_gate[0]), GLA_DK)
    la_b = heads(log_decay(lr_b, w_gate2[1], b_gate[1]), GLA_DK)

    flip = lambda t: jnp.flip(t, axis=2)
    o_f = gla_chunked(qh, kh, vh, la_f, False)
    o_b = flip(gla_chunked(flip(qh), flip(kh), flip(vh), flip(la_b), True))
    o = (o_f + o_b).transpose(0, 2, 1, 3)
    o = rms_norm(o, o_norm_g).reshape(B, S, GLA_V).astype(h.dtype) * jax.nn.silu(g)
    return o @ w_out


def t5_bucket(rel):
    nb = REL_BUCKETS // 2
    max_exact = nb // 2
    ret = (rel > 0).astype(jnp.int32) * nb
    n = jnp.abs(rel)
    nf = jnp.maximum(n, 1).astype(jnp.float32)
    large = max_exact + (jnp.log(nf / max_exact) / math.log(REL_MAX_DIST / max_exact) * (nb - max_exact)).astype(jnp.int32)
    large = jnp.minimum(large, nb - 1)
    return ret + jnp.where(n < max_exact, n, large)


def diff_mixer(h, w_in, qk_norm_g, lam_vecs, subln_g, w_out, rel_table, lam_init):
    B, S, _ = h.shape
    proj = h @ w_in
    q, k, v = jnp.split(proj, [DIFF_QK, 2 * DIFF_QK], axis=-1)
    q = rms_norm(q.reshape(B, S, DIFF_HEADS, 2, DIFF_DH), qk_norm_g[0]).astype(jnp.float32) * (DIFF_DH ** -0.5)
    k = rms_norm(k.reshape(B, S, DIFF_HEADS, 2, DIFF_DH), qk_norm_g[1]).astype(jnp.float32)
    q1 = q[..., 0, :].transpose(0, 2, 1, 3)
    q2 = q[..., 1, :].transpose(0, 2, 1, 3)
    k1 = k[..., 0, :].transpose(0, 2, 1, 3)
    k2 = k[..., 1, :].transpose(0, 2, 1, 3)
    vh = v.reshape(B, S, DIFF_HEADS, 2 * DIFF_DH).transpose(0, 2, 1, 3).astype(jnp.float32)

    lv = lam_vecs.astype(jnp.float32)
    lam = jnp.exp(jnp.sum(lv[0] * lv[1])) - jnp.exp(jnp.sum(lv[2] * lv[3])) + lam_init
    table = rel_table.astype(jnp.float32)
    nblk = S // Q_BLOCK
    kpos = jnp.arange(S, dtype=jnp.int32)

    def blocks(t):
        return jnp.moveaxis(t.reshape(B, DIFF_HEADS, nblk, Q_BLOCK, DIFF_DH), 2, 0)

    def attend(args):
        q1b, q2b, start = args
        qpos = start + jnp.arange(Q_BLOCK, dtype=jnp.int32)
        bias = jnp.transpose(table[t5_bucket(kpos[None, :] - qpos[:, None])], (2, 0, 1))
        p1 = jax.nn.softmax(jnp.einsum('bhqd,bhkd->bhqk', q1b, k1) + bias, axis=-1)
        p2 = jax.nn.softmax(jnp.einsum('bhqd,bhkd->bhqk', q2b, k2) + bias, axis=-1)
        return jnp.einsum('bhqk,bhkv->bhqv', p1 - lam * p2, vh)

    starts = jnp.arange(nblk, dtype=jnp.int32) * Q_BLOCK
    out = lax.map(attend, (blocks(q1), blocks(q2), starts))
    out = out.transpose(1, 0, 3, 2, 4).reshape(B, S, DIFF_HEADS, 2 * DIFF_DH)
    out = rms_norm(out, subln_g) * (1.0 - lam_init)
    return out.reshape(B, S, DIFF_V).astype(h.dtype) @ w_out


def setup_inputs(seed: int = 0) -> dict:
    key = jax.random.key(seed)
    ks = jax.random.split(key, 16)
    f32 = jnp.float32
    nrm = lambda k, shape, s: jax.random.normal(k, shape, f32) * s
    return {
        "x": nrm(ks[0], (BATCH, SEQ, D_MODEL), 1.0),
        "norm_g": 1.0 + nrm(ks[1], (DEPTH, 3, D_MODEL), 0.02),
        "ffn_w_gu": nrm(ks[2], (DEPTH, 2, D_MODEL, 2 * D_FF), D_MODEL ** -0.5),
        "ffn_w_down": nrm(ks[3], (DEPTH, 2, D_FF, D_MODEL), D_FF ** -0.5),
        "gla_w_in": nrm(ks[4], (N_A_LAYERS, D_MODEL, GLA_IN), D_MODEL ** -0.5),
        "gla_w_gate2": nrm(ks[5], (N_A_LAYERS, 2, GLA_GATE_RANK, GLA_QK), GLA_GATE_RANK ** -0.5),
        "gla_b_gate": nrm(ks[6], (N_A_LAYERS, 2, GLA_QK), 0.1),
        "gla_o_norm_g": 1.0 + nrm(ks[7], (N_A_LAYERS, GLA_DV), 0.02),
        "gla_w_out": nrm(ks[8], (N_A_LAYERS, GLA_V, D_MODEL), GLA_V ** -0.5),
        "diff_w_in": nrm(ks[9], (N_B_LAYERS, D_MODEL, DIFF_IN), D_MODEL ** -0.5),
        "diff_qk_norm_g": 1.0 + nrm(ks[10], (N_B_LAYERS, 2, DIFF_DH), 0.02),
        "diff_lambda": nrm(ks[11], (N_B_LAYERS, 4, DIFF_DH), 0.1),
        "diff_subln_g": 1.0 + nrm(ks[12], (N_B_LAYERS, 2 * DIFF_DH), 0.02),
        "diff_w_out": nrm(ks[13], (N_B_LAYERS, DIFF_V, D_MODEL), DIFF_V ** -0.5),
        "rel_bias_table": nrm(ks[14], (REL_BUCKETS, DIFF_HEADS), 0.5),
    }


def reference(x, norm_g, ffn_w_gu, ffn_w_down, gla_w_in, gla_w_gate2, gla_b_gate, gla_o_norm_g,
              gla_w_out, diff_w_in, diff_qk_norm_g, diff_lambda, diff_subln_g, diff_w_out, rel_bias_table):
    for i in range(DEPTH):
        j = i // N_MIXERS
        x = x + 0.5 * swiglu_ffn(rms_norm(x, norm_g[i, 0]), ffn_w_gu[i, 0], ffn_w_down[i, 0])
        h = rms_norm(x, norm_g[i, 1])
        if i % N_MIXERS == 0:
            x = x + gla_mixer(h, gla_w_in[j], gla_w_gate2[j], gla_b_gate[j], gla_o_norm_g[j], gla_w_out[j])
        else:
            lam_init = 0.8 - 0.6 * math.exp(-0.3 * i)
            x = x + diff_mixer(h, diff_w_in[j], diff_qk_norm_g[j], diff_lambda[j], diff_subln_g[j],
                               diff_w_out[j], rel_bias_table, lam_init)
        x = x + 0.5 * swiglu_ffn(rms_norm(x, norm_g[i, 2]), ffn_w_gu[i, 1], ffn_w_down[i, 1])
    return x
```

```python
import numpy as np
import concourse.bass as bass
import concourse.mybir as mybir

F32 = mybir.dt.float32
BF16 = mybir.dt.bfloat16
ALU = mybir.AluOpType
AF = mybir.ActivationFunctionType
DTSIZE = {F32: 4, BF16: 2}

CELL = 128
SEM_LIMIT = 12000
N_DMA_SEMS = 24
SAME_ENGINE_SYNC = True
SB_BASE = 16512
SB_END = 229344


class Acc:
    __slots__ = ("ap", "rng")

    def __init__(self, ap, rng):
        self.ap = ap
        self.rng = rng

    def re(self, pattern, **kw):
        return Acc(self.ap.rearrange(pattern, **kw), self.rng)


class Buf:
    def __init__(self, prog, name, shape, dtype, space, base, handle):
        self.prog = prog
        self.name = name
        self.shape = list(shape)
        self.dtype = dtype
        self.esz = DTSIZE[dtype]
        self.space = space
        self.base = base
        self.t = handle
        fs = self.shape[1:]
        st = [1] * len(fs)
        for i in range(len(fs) - 2, -1, -1):
            st[i] = st[i + 1] * fs[i + 1]
        self.strides = st

    def __getitem__(self, idx):
        if not isinstance(idx, tuple):
            idx = (idx,)
        idx = list(idx) + [slice(None)] * (len(self.shape) - len(idx))
        ap = self.t[tuple(idx)]
        fidx = idx[1:]
        fs = self.shape[1:]
        lohi = []
        for k, ix in enumerate(fidx):
            if isinstance(ix, slice):
                lo = 0 if ix.start is None else ix.start
                hi = fs[k] if ix.stop is None else ix.stop
                assert ix.step in (None, 1)
            else:
                lo, hi = ix, ix + 1
            assert 0 <= lo < hi <= fs[k], (self.name, idx, self.shape)
            lohi.append((lo, hi))
        n = len(fs)
        k = n - 1
        run = 1
        while k >= 0 and lohi[k] == (0, fs[k]):
            run *= fs[k]
            k -= 1
        ranges = []
        if k < 0:
            ranges.append((0, run))
        else:
            outer = lohi[:k]
            lo_k, hi_k = lohi[k]
            seg = (hi_k - lo_k) * self.strides[k]

            def rec(d, off):
                if d == k:
                    s = off + lo_k * self.strides[k]
                    ranges.append((s, s + seg))
                    return
                for i in range(outer[d][0], outer[d][1]):
                    rec(d + 1, off + i * self.strides[d])
            rec(0, 0)
        if self.space == "ps":
            b = self.base // 2048
            return Acc(ap, [("ps", b, b + 1)])
        rng = []
        for (s, e) in ranges:
            b0 = self.base + s * self.esz
            b1 = self.base + e * self.esz
            rng.append((self.space, b0 // CELL, (b1 - 1) // CELL + 1))
        return Acc(ap, rng)


class Op:
    __slots__ = ("eng", "fn", "deps", "is_dma", "sig", "needs_sig", "idx", "tag")

    def __init__(self, eng, fn, is_dma, tag):
        self.eng = eng
        self.fn = fn
        self.deps = []
        self.is_dma = is_dma
        self.sig = None
        self.needs_sig = False
        self.tag = tag


class Prog:
    ENGS = ("pe", "act", "dve", "pool", "sp")

    def __init__(self, nc):
        self.nc = nc
        self.ops = {e: [] for e in self.ENGS}
        self.nops = 0
        self.sb_off = SB_BASE
        self.ps_bank = 0
        self.lastw = {"sb": {}, "ps": {}}
        self.readers = {"sb": {}, "ps": {}}
        self.sems = []
        self.final_waits = []
        self.n_bufs = 0

    def sbuf(self, name, shape, dtype, at=None, align=CELL):
        esz = DTSIZE[dtype]
        nbytes = int(np.prod(shape[1:])) * esz
        if at is None:
            off = (self.sb_off + align - 1) // align * align
            self.sb_off = off + nbytes
            assert self.sb_off <= SB_END, (name, self.sb_off)
        else:
            off = at
        self.n_bufs += 1
        h = self.nc.alloc_sbuf_tensor_at(f"{name}_{self.n_bufs}", list(shape), dtype, offset=off)
        return Buf(self, name, shape, dtype, "sb", off, h)

    def psum_bank(self, name):
        b = self.ps_bank
        self.ps_bank += 1
        assert b < 8
        h = self.nc.alloc_psum_tensor(f"{name}_{b}", [128, 512], F32)
        return Buf(self, name, [128, 512], F32, "ps", b * 2048, h)

    def add(self, eng, fn, reads=(), writes=(), is_dma=False, tag=None):
        op = Op(eng, fn, is_dma, tag)
        op.idx = self.nops
        self.nops += 1
        deps = set()
        ps_reads = [a for a in reads if a.rng and a.rng[0][0] == "ps"]
        if ps_reads:
            reads = [a for a in reads if not (a.rng and a.rng[0][0] == "ps")]
            writes = list(writes) + ps_reads
        for a in reads:
            for (sp, c0, c1) in a.rng:
                lw = self.lastw[sp]
                for c in range(c0, c1):
                    w = lw.get(c)
                    if w is not None:
                        deps.add(w)
        for a in writes:
            for (sp, c0, c1) in a.rng:
                lw = self.lastw[sp]
                rd = self.readers[sp]
                for c in range(c0, c1):
                    w = lw.get(c)
                    if w is not None:
                        deps.add(w)
                    r = rd.get(c)
                    if r:
                        deps.update(r)
        for a in reads:
            for (sp, c0, c1) in a.rng:
                rd = self.readers[sp]
                for c in range(c0, c1):
                    l = rd.get(c)
                    if l is None:
                        rd[c] = [op]
                    elif l[-1] is not op:
                        l.append(op)
        for a in writes:
            for (sp, c0, c1) in a.rng:
                lw = self.lastw[sp]
                rd = self.readers[sp]
                for c in range(c0, c1):
                    lw[c] = op
                    if c in rd:
                        rd[c] = None
        deps.discard(op)
        for d in deps:
            if d.eng == eng and not d.is_dma:
                if eng == "pe" or not SAME_ENGINE_SYNC:
                    continue
            op.deps.append(d)
            d.needs_sig = True
        self.ops[eng].append(op)
        return op

    def matmul(self, out, lhsT, rhs, start=True, stop=True, **kw):
        return self.add("pe", lambda e: e.matmul(out.ap, lhsT.ap, rhs.ap, start=start, stop=stop, **kw),
                        reads=[lhsT, rhs], writes=[out])

    def transpose(self, out, in_, ident):
        return self.add("pe", lambda e: e.transpose(out.ap, in_.ap, ident.ap),
                        reads=[in_, ident], writes=[out])

    def act(self, out, in_, func, bias=None, scale=1.0, accum_out=None, eng="act"):
        reads = [in_]
        kw = {}
        if bias is not None:
            if isinstance(bias, Acc):
                reads.append(bias)
                kw["bias"] = bias.ap
            else:
                kw["bias"] = bias
        if isinstance(scale, Acc):
            reads.append(scale)
            kw["scale"] = scale.ap
        else:
            kw["scale"] = scale
        writes = [out]
        if accum_out is not None:
            writes.append(accum_out)
            kw["accum_out"] = accum_out.ap
        return self.add(eng, lambda e: e.activation(out.ap, in_.ap, func, **kw), reads=reads, writes=writes)

    def tt(self, out, in0, in1, op, eng="dve"):
        return self.add(eng, lambda e: e.tensor_tensor(out.ap, in0.ap, in1.ap, op), reads=[in0, in1], writes=[out])

    def ts(self, out, in0, s1, s2, op0, op1=None, eng="dve"):
        reads = [in0]
        a1 = s1.ap if isinstance(s1, Acc) else s1
        a2 = s2.ap if isinstance(s2, Acc) else s2
        if isinstance(s1, Acc):
            reads.append(s1)
        if isinstance(s2, Acc):
            reads.append(s2)
        if op1 is None:
            return self.add(eng, lambda e: e.tensor_scalar(out.ap, in0.ap, a1, None, op0), reads=reads, writes=[out])
        return self.add(eng, lambda e: e.tensor_scalar(out.ap, in0.ap, a1, a2, op0, op1), reads=reads, writes=[out])

    def stt(self, out, in0, scalar, in1, op0, op1, eng="dve"):
        reads = [in0, in1]
        sc = scalar.ap if isinstance(scalar, Acc) else scalar
        if isinstance(scalar, Acc):
            reads.append(scalar)
        return self.add(eng, lambda e: e.scalar_tensor_tensor(out.ap, in0.ap, sc, in1.ap, op0, op1),
                        reads=reads, writes=[out])

    def copy(self, out, in_, eng="dve"):
        if eng == "act":
            return self.add("act", lambda e: e.copy(out.ap, in_.ap), reads=[in_], writes=[out])
        return self.add(eng, lambda e: e.tensor_copy(out.ap, in_.ap), reads=[in_], writes=[out])

    def recip(self, out, in_):
        return self.add("dve", lambda e: e.reciprocal(out.ap, in_.ap), reads=[in_], writes=[out])

    def memset(self, out, val, eng="pool"):
        return self.add(eng, lambda e: e.memset(out.ap, val), reads=[], writes=[out])

    def dma(self, out, in_, eng="sp", final=False, **kw):
        reads = [in_] if isinstance(in_, Acc) else []
        writes = [out] if isinstance(out, Acc) else []
        oap = out.ap if isinstance(out, Acc) else out
        iap = in_.ap if isinstance(in_, Acc) else in_
        op = self.add(eng, lambda e: e.dma_start(out=oap, in_=iap, **kw), reads=reads, writes=writes, is_dma=True)
        if final:
            op.needs_sig = True
            self.final_waits.append(op)
        return op

    def _new_sem(self, name):
        cm = self.nc.semaphore(f"{name}_{len(self.sems)}")
        h = cm.__enter__()
        self.sems.append(cm)
        return h

    def emit(self):
        nc = self.nc
        dma_sems = [[self._new_sem("dma"), 0, None] for _ in range(N_DMA_SEMS)]
        dma_rr = 0
        all_dma = sorted([op for e in self.ENGS for op in self.ops[e] if op.is_dma], key=lambda o: o.idx)
        for op in all_dma:
            slot = dma_sems[dma_rr % N_DMA_SEMS]
            dma_rr += 1
            if slot[1] + 16 > SEM_LIMIT:
                slot[0] = self._new_sem("dma")
                slot[1] = 0
                slot[2] = None
            if slot[2] is not None:
                prev = slot[2]
                if prev not in op.deps:
                    op.deps.append(prev)
            slot[1] += 16
            op.sig = (slot[0], slot[1])
            slot[2] = op
        for e in self.ENGS:
            sem = None
            cnt = 0
            for op in self.ops[e]:
                if op.is_dma or not op.needs_sig:
                    continue
                if sem is None or cnt + 1 > SEM_LIMIT:
                    sem = self._new_sem(e)
                    cnt = 0
                cnt += 1
                op.sig = (sem, cnt)
        nwaits = {e: 0 for e in self.ENGS}

        def emit_engine(ename, eng):
            seen = {}
            for op in self.ops[ename]:
                need = {}
                for d in op.deps:
                    s, v = d.sig
                    if seen.get(s, 0) >= v:
                        continue
                    if need.get(s, (None, 0))[1] < v:
                        need[s] = (s, v)
                for (s, v) in need.values():
                    eng.wait_ge(s, v)
                    seen[s] = v
                    nwaits[ename] += 1
                ins = op.fn(eng)
                if op.sig is not None:
                    ins.then_inc(op.sig[0], 16 if op.is_dma else 1)
            if ename == "sp":
                for op in self.final_waits:
                    s, v = op.sig
                    if seen.get(s, 0) < v:
                        eng.wait_ge(s, v)
                        seen[s] = v

        with nc.Block() as block:
            @block.tensor
            def _(e):
                emit_engine("pe", e)

            @block.scalar
            def _(e):
                emit_engine("act", e)

            @block.vector
            def _(e):
                emit_engine("dve", e)

            @block.gpsimd
            def _(e):
                emit_engine("pool", e)

            @block.sync
            def _(e):
                emit_engine("sp", e)
        self.stats = {e: (len(self.ops[e]), nwaits[e]) for e in self.ENGS}
        self.stats["sems"] = len(self.sems)
        return self.stats

from concourse.bass_utils import run_bass_kernel_spmd

D_MODEL = 1024
SEQ = 2048
DEPTH = 4
D_FF = 2816
NF = D_FF // 128
EPS = 1e-6
FFN_GROUPS = [(0, 4), (4, 4), (8, 4), (12, 4), (16, 3), (19, 3)]
NT = SEQ // 512


class K:
    def __init__(self, nc, n_seq, plan, decl_depth=DEPTH, **dbg):
        self.nc = nc
        for k_, v_ in dbg.items():
            setattr(self, k_, v_)
        self.n_seq = n_seq
        self.plan = plan
        P = self.P = Prog(nc)
        d = self.d = {}

        def din(name, shape):
            d[name] = nc.dram_tensor(name, list(shape), F32, kind="ExternalInput").ap()
        din("x", [n_seq, SEQ, D_MODEL])
        din("norm_g", [DEPTH * 3 * 8, 128])
        din("ffn_w_gu", [decl_depth, 2, D_MODEL, 2 * D_FF])
        din("ffn_w_down", [decl_depth, 2, D_FF, D_MODEL])
        din("c_ident", [128, 128])
        din("c_tri", [4, 128, 128])
        din("gla_w_in", [max(1, decl_depth // 2), D_MODEL, 3104])
        din("gla_w_gate2", [max(1, decl_depth // 2), 2, 16, 512])
        din("gla_b_gate", [max(1, decl_depth // 2), 2, 512])
        din("gla_o_norm_g", [4, 128])
        din("gla_w_out", [max(1, decl_depth // 2), 1024, 1024])
        din("c_oh", [32, 1281])
        din("diff_w_in", [max(1, decl_depth // 2), D_MODEL, 3072])
        din("diff_qk_norm_g", [max(1, decl_depth // 2), 2, 64])
        din("diff_lambda", [max(1, decl_depth // 2), 256])
        din("diff_subln_g", [max(1, decl_depth // 2), 128])
        din("diff_w_out", [max(1, decl_depth // 2), 1024, 1024])
        din("rel_bias_table", [32, 8])
        self.ebscr = nc.dram_tensor("ebscr", [8 * 128 * 1279], F32)
        self.y = nc.dram_tensor("y", [n_seq, SEQ, D_MODEL], F32, kind="ExternalOutput").ap()

        self.xT = P.sbuf("xT", [128, 8, SEQ], F32)
        self.hT = P.sbuf("hT", [128, 8, SEQ], BF16)
        self.ident = P.sbuf("ident", [128, 128], F32)
        self.ones_bf = P.sbuf("ones_bf", [128, 128], BF16)
        self.gn = P.sbuf("gn", [128, DEPTH * 3 * 8], F32)
        self.tri = [P.sbuf(f"tri{i}", [128, 128], F32) for i in range(4)]
        self.eps_t = P.sbuf("eps_t", [128, 1], F32)
        self.blk2 = P.sbuf("blk2", [128, 128], BF16)
        self.cfar = P.sbuf("cfar", [128, 16], F32)
        self.one_t = P.sbuf("one_t", [128, 1], F32)
        self.gno = P.sbuf("gno", [128, 4], F32)
        self.wslot_off = []
        for i in range(2):
            b = P.sbuf(f"wslot{i}", [128, 12288], BF16)
            self.wslot_off.append(b.base)
        self.wv = []
        for i in range(2):
            o = self.wslot_off[i]
            self.wv.append(dict(
                gate=P.sbuf(f"wg{i}", [128, 8, 512], BF16, at=o),
                up=P.sbuf(f"wu{i}", [128, 8, 512], BF16, at=o + 8192),
                down=P.sbuf(f"wd{i}", [128, 4, 1024], BF16, at=o + 16384),
            ))
        self.gw_in = [P.sbuf(f"gwin{i}", [128, 8, 768], BF16, at=self.wslot_off[i]) for i in range(2)]
        self.gw_out = [P.sbuf(f"gwout{i}", [128, 2, 1024], BF16, at=self.wslot_off[i] + 12288) for i in range(2)]
        self.sq = [P.sbuf(f"sq{i}", [128, 512], BF16) for i in range(2)]
        self.lnt = [P.sbuf(f"lnt{i}", [128, 512], F32) for i in range(2)]
        arena = P.sb_off
        print("arena base", arena, "bytes avail", SB_END - arena, flush=True)
        self.stage = [P.sbuf(f"stage{i}", [128, 1024], F32) for i in range(2)]
        P.sb_off = arena
        self.aT = [P.sbuf(f"aT{i}", [128, 4, 512], BF16) for i in range(2)]
        self.sg = [P.sbuf(f"sg{i}", [128, 512], F32) for i in range(2)]
        P.sb_off = arena
        self.g_qT = P.sbuf("g_qT", [128, SEQ], F32)
        self.g_kT = P.sbuf("g_kT", [128, SEQ], F32)
        self.g_vtok = P.sbuf("g_vtok", [128, 16, 256], BF16)
        self.g_oT = P.sbuf("g_oT", [128, 2, SEQ], F32)
        self.g_lrT = P.sbuf("g_lrT", [64, SEQ], BF16)
        self.g_onb = P.sbuf("g_onb", [128, 2, 512], BF16)
        gla_end = P.sb_off
        P.sb_off = self.wslot_off[0] + 16384
        self.g_e1 = [P.sbuf(f"g_e1{i}", [128, 128], F32) for i in range(2)]
        self.g_la = [P.sbuf(f"g_la{i}", [128, 128], F32) for i in range(2)]
        self.g_eq = [P.sbuf(f"g_eq{i}", [128, 128], F32) for i in range(2)]
        self.g_ek = [P.sbuf(f"g_ek{i}", [128, 128], F32) for i in range(2)]
        self.g_ekd = [P.sbuf(f"g_ekd{i}", [128, 128], F32) for i in range(2)]
        self.g_w2 = P.sbuf("g_w2", [64, 512], BF16)
        self.g_wlr = P.sbuf("g_wlr", [128, 8, 64], BF16)
        assert P.sb_off <= self.wslot_off[0] + 24576
        P.sb_off = self.wslot_off[1] + 16384
        self.g_qt = [P.sbuf(f"g_qt{i}", [128, 128], BF16) for i in range(2)]
        self.g_kt = [P.sbuf(f"g_kt{i}", [128, 128], BF16) for i in range(2)]
        self.g_kd = [P.sbuf(f"g_kd{i}", [128, 128], BF16) for i in range(2)]
        self.g_sm = [P.sbuf(f"g_sm{i}", [128, 128], BF16) for i in range(2)]
        self.g_S = [P.sbuf(f"g_S{i}", [128, 256], F32) for i in range(2)]
        self.g_Sbf = [P.sbuf(f"g_Sbf{i}", [128, 256], BF16) for i in range(2)]
        self.g_sg = P.sbuf("g_sg", [128, 512], F32)
        assert P.sb_off <= self.wslot_off[1] + 24576
        P.sb_off = gla_end
        print("gla arena end", P.sb_off, flush=True)
        P.sb_off = arena
        self.d_on = P.sbuf("d_on", [128, 8, SEQ], BF16)
        self.d_qn = P.sbuf("d_qn", [128, SEQ], BF16)
        self.d_kn = P.sbuf("d_kn", [128, SEQ], BF16)
        self.d_vtok = P.sbuf("d_vtok", [128, 16, 128], BF16)
        self.d_eb = P.sbuf("d_eb", [128, 1152], F32)
        self.d_rz = [P.sbuf(f"d_rz{i}", [128, 512], F32) for i in range(2)]
        print("diff arena end", P.sb_off, flush=True)
        dend = P.sb_off
        P.sb_off = self.wslot_off[0] + 8192
        self.d_e = [P.sbuf(f"d_e{i}", [128, 512], F32) for i in range(3)]
        self.d_pt = [P.sbuf(f"d_pt{i}", [128, 512], BF16) for i in range(4)]
        self.d_t = [P.sbuf(f"d_t{i}", [128, 512], F32) for i in range(3)]
        assert P.sb_off <= self.wslot_off[0] + 24576
        P.sb_off = self.wslot_off[1] + 8192
        self.d_rs = P.sbuf("d_rs", [128, 512], F32)
        self.d_small = P.sbuf("d_small", [128, 16], F32)
        self.d_lv = P.sbuf("d_lv", [128, 256], F32)
        self.d_lp = P.sbuf("d_lp", [128, 128], F32)
        assert P.sb_off <= self.wslot_off[1] + 24576
        P.sb_off = max(dend, gla_end)
        self.dw_in = [P.sbuf(f"dwin{i}", [128, 8, 384], BF16, at=self.wslot_off[i]) for i in range(2)]
        self.dw_out = [P.sbuf(f"dwout{i}", [128, 4, 1024], BF16, at=self.wslot_off[i]) for i in range(2)]
        self.PS = [P.psum_bank("ps") for _ in range(8)]
        self.load_q = []
        self.load_issued = 0
        self.tasks_done = 0
        self.in_mixer = False
        self.task_slot = {}

    def prefetch(self):
        self.tasks_done += 1
        self.pump()

    def pump(self):
        while self.load_issued < min(len(self.load_q), self.tasks_done + 2):
            k = self.load_issued
            fn, full = self.load_q[k]
            if full and self.in_mixer:
                return
            self.load_issued += 1
            self.task_slot[k] = k % 2
            fn(k % 2)

    def setup(self):
        P = self.P
        P.dma(self.ident[:], self.d["c_ident"], eng="sp")
        P.memset(self.ones_bf[:], 1.0, eng="pool")
        st = self.stage[0]
        P.dma(st[0:96, 0:128], self.d["norm_g"], eng="sp")
        ps = self.PS[0]
        P.transpose(ps[:, 0:96], st[0:96, 0:128], self.ident[0:96, 0:96])
        P.copy(self.gn[:], ps[:, 0:96], eng="dve")
        for q in range(4):
            P.dma(self.tri[q][:], self.d["c_tri"][q], eng="sp")
        st1 = self.stage[1]
        P.dma(st1[0:4, 0:128], self.d["gla_o_norm_g"], eng="sp")
        ps1 = self.PS[1]
        P.transpose(ps1[:, 0:4], st1[0:4, 0:128], self.ident[0:4, 0:4])
        P.copy(self.gno[:], ps1[:, 0:4], eng="dve")


    def setup_bias(self):
        P = self.P
        PS = self.PS
        st = self.stage
        P.memset(self.blk2[:], 0.0, eng="pool")
        P.memset(self.blk2[0:64, 0:64], 1.0, eng="pool")
        P.memset(self.blk2[64:128, 64:128], 1.0, eng="pool")
        oh = st[0]
        oh2 = st[1]
        P.dma(oh[0:32, 0:1024], self.d["c_oh"][:, 0:1024], eng="sp")
        P.dma(oh2[0:32, 0:257], self.d["c_oh"][:, 1024:1281], eng="sp")
        P.dma(oh2[0:32, 512:520], self.d["rel_bias_table"], eng="sp")
        P.memset(oh2[0:32, 640:768], 1.0, eng="dve")
        sbs = getattr(self, "dbg_sbs", 99)
        if sbs <= 0:
            return
        for h in range(8):
            tb = oh2[0:32, 768:896]
            P.ts(tb, oh2[0:32, 640:768], oh2[0:32, 512 + h:513 + h], None, ALU.mult)
            b0, b1, b2 = PS[(3 * h) % 8], PS[(3 * h + 1) % 8], PS[(3 * h + 2) % 8]
            if sbs <= 1:
                continue
            P.matmul(b0[:, 0:512], tb, oh[0:32, 0:512])
            P.matmul(b1[:, 0:512], tb, oh[0:32, 512:1024])
            P.matmul(b2[:, 0:257], tb, oh2[0:32, 0:257])
            if sbs <= 2:
                continue
            e = self.g_oT
            P.act(e[:, 0, 0:512], b0[:, 0:512], AF.Exp)
            P.act(e[:, 0, 512:1024], b1[:, 0:512], AF.Exp)
            P.act(e[:, 0, 1024:1279], b2[:, 0:255], AF.Exp)
            P.copy(self.cfar[:, 2 * h:2 * h + 2], b2[:, 255:257], eng="dve")
            if sbs <= 3:
                continue
            dst = bass.AP(self.ebscr, h * 128 * 1279, [[1279, 128], [1, 1279]])
            op = P.dma(dst, e[:, 0, 0:1279], eng="sp")
            self.eb_writes.append(op)

    def load_x(self, s):
        P = self.P
        for b in range(SEQ // 128):
            st = self.stage[b % 2]
            P.dma(st[:], self.d["x"][s, b * 128:(b + 1) * 128, :], eng="sp")
            for half in range(2):
                ps = self.PS[(2 * b + half) % 4]
                for cc in range(4):
                    c = half * 4 + cc
                    P.transpose(ps[:, cc * 128:(cc + 1) * 128], st[:, c * 128:(c + 1) * 128], self.ident[:])
                dst = self.xT[:, half * 4:half * 4 + 4, b * 128:(b + 1) * 128]
                src = ps[:].re("p (a b) -> p a b", a=4)
                if half == 0:
                    P.copy(dst, src, eng="dve")
                else:
                    P.copy(dst, src, eng="act")

    def store_x(self, s):
        P = self.P
        for b in range(SEQ // 128):
            st = self.stage[b % 2]
            for half in range(2):
                ps = self.PS[(2 * b + half) % 4]
                for cc in range(4):
                    c = half * 4 + cc
                    P.transpose(ps[:, cc * 128:(cc + 1) * 128], self.xT[:, c, b * 128:(b + 1) * 128], self.ident[:])
                dst = st[:, half * 512:(half + 1) * 512]
                if half == 0:
                    P.copy(dst, ps[:], eng="dve")
                else:
                    P.copy(dst, ps[:], eng="act")
            P.dma(self.y[s, b * 128:(b + 1) * 128, :], st[:], eng="sp", final=True)

    def rmsnorm_all(self, gcol):
        P = self.P
        for t in range(NT):
            tsl = slice(t * 512, (t + 1) * 512)
            ss = self.PS[4 + (t % 2) * 2]
            rs = self.PS[5 + (t % 2) * 2]
            for c in range(8):
                sq = self.sq[c % 2]
                P.act(sq[:], self.xT[:, c, tsl], AF.Square)
                P.matmul(ss[:], self.ones_bf[:], sq[:], start=(c == 0), stop=(c == 7))
            lt = self.lnt[t % 2]
            P.act(lt[:], ss[:], AF.Ln, bias=self.eps_t[:], scale=1.0 / D_MODEL)
            P.act(rs[:], lt[:], AF.Exp, scale=-0.5)
            for c in range(8):
                P.stt(self.hT[:, c, tsl], self.xT[:, c, tsl], self.gn[:, gcol + c:gcol + c + 1], rs[:],
                      ALU.mult, ALU.mult)

    def ffn_load(self, i, j, gi, slot):
        P = self.P
        f0, G = FFN_GROUPS[gi]
        wv = self.wv[slot]
        wgu = self.d["ffn_w_gu"][i, j].rearrange("(c p) n -> p c n", p=128)
        P.dma(wv["gate"][:, :, 0:G * 128], wgu[:, :, f0 * 128:(f0 + G) * 128], eng="pool")
        P.dma(wv["up"][:, :, 0:G * 128], wgu[:, :, D_FF + f0 * 128:D_FF + (f0 + G) * 128], eng="pool")
        wd = self.d["ffn_w_down"][i, j][f0 * 128:(f0 + G) * 128, :].rearrange("(g p) d -> p g d", p=128)
        P.dma(wv["down"][:, 0:G, :], wd, eng="pool")

    def ffn_tasks(self, i, j):
        return [((lambda slot, gi=gi: self.ffn_load(i, j, gi, slot)), True) for gi in range(len(FFN_GROUPS))]

    def ffn(self, i, j, tbase):
        P = self.P
        gcol = (i * 3 + (0 if j == 0 else 2)) * 8
        self.rmsnorm_all(gcol)
        ngroups = len(FFN_GROUPS)
        slots = self.task_slot
        units = [(gi, t, fi) for gi in range(ngroups) for t in range(NT) for fi in range(FFN_GROUPS[gi][1])]
        ucount = [0]
        dcount = [0]

        def U(n):
            gi, t, fi = units[n]
            wv = self.wv[slots[tbase + gi]]
            tsl = slice(t * 512, (t + 1) * 512)
            pg = self.PS[(n % 2) * 2]
            pu = self.PS[(n % 2) * 2 + 1]
            for c in range(8):
                P.matmul(pg[:], wv["gate"][:, c, fi * 128:(fi + 1) * 128], self.hT[:, c, tsl], start=(c == 0), stop=(c == 7))
            for c in range(8):
                P.matmul(pu[:], wv["up"][:, c, fi * 128:(fi + 1) * 128], self.hT[:, c, tsl], start=(c == 0), stop=(c == 7))

        def E(n):
            gi, t, fi = units[n]
            G = FFN_GROUPS[gi][1]
            pg = self.PS[(n % 2) * 2]
            pu = self.PS[(n % 2) * 2 + 1]
            sg = self.sg[n % 2]
            a = self.aT[(gi * NT + t) % 2]
            P.act(sg[:], pg[:], AF.Silu)
            P.tt(a[:, fi, :], pu[:], sg[:], ALU.mult)
            if fi == G - 1:
                wv = self.wv[slots[tbase + gi]]
                tsl = slice(t * 512, (t + 1) * 512)
                for dc in range(8):
                    bank = self.PS[4 + dcount[0] % 4]
                    dcount[0] += 1
                    for f in range(G):
                        P.matmul(bank[:], wv["down"][:, f, dc * 128:(dc + 1) * 128], a[:, f, :], start=(f == 0), stop=(f == G - 1))
                    P.stt(self.xT[:, dc, tsl], bank[:], 0.5, self.xT[:, dc, tsl], ALU.mult, ALU.add)
                if t == NT - 1:
                    self.prefetch()

        N = len(units)
        U(0)
        for n in range(N):
            if n + 1 < N:
                U(n + 1)
            E(n)


    def gla_load(self, j, h, slot):
        P = self.P
        win = self.gw_in[slot]
        wout = self.gw_out[slot]
        W = self.d["gla_w_in"][j].rearrange("(c p) n -> p c n", p=128)
        P.dma(win[:, :, 0:128], W[:, :, h * 128:(h + 1) * 128], eng="pool")
        P.dma(win[:, :, 128:256], W[:, :, 512 + h * 128:512 + (h + 1) * 128], eng="pool")
        P.dma(win[:, :, 256:512], W[:, :, 1024 + h * 256:1024 + (h + 1) * 256], eng="pool")
        P.dma(win[:, :, 512:768], W[:, :, 2048 + h * 256:2048 + (h + 1) * 256], eng="pool")
        wo = self.d["gla_w_out"][j][h * 256:(h + 1) * 256, :].rearrange("(g p) d -> p g d", p=128)
        P.dma(wout[:], wo, eng="pool")

    def gla_tasks(self, i):
        j = i // 2
        return [((lambda slot, h=h: self.gla_load(j, h, slot)), False) for h in range(4)]

    def gla(self, i, tb):
        P = self.P
        j = i // 2
        PS = self.PS
        self.rmsnorm_all((i * 3 + 1) * 8)
        W = self.d["gla_w_in"][j].rearrange("(c p) n -> p c n", p=128)
        P.memset(self.g_wlr[:], 0.0, eng="pool")
        P.dma(self.g_wlr[:, :, 0:16], W[:, :, 3072:3088], eng="pool")
        P.dma(self.g_wlr[:, :, 32:48], W[:, :, 3088:3104], eng="pool")
        P.dma(self.g_w2[0:16, :], self.d["gla_w_gate2"][j, 0], eng="pool")
        P.dma(self.g_w2[16:17, :], self.d["gla_b_gate"][j, 0:1, :], eng="pool")
        P.dma(self.g_w2[32:48, :], self.d["gla_w_gate2"][j, 1], eng="pool")
        P.dma(self.g_w2[48:49, :], self.d["gla_b_gate"][j, 1:2, :], eng="pool")
        P.memset(self.g_lrT[:], 1.0, eng="pool")
        for t in range(NT):
            tsl = slice(t * 512, (t + 1) * 512)
            ps = PS[t % 2]
            for c in range(8):
                P.matmul(ps[0:48, :], self.g_wlr[:, c, 0:48], self.hT[:, c, tsl], start=(c == 0), stop=(c == 7))
            P.copy(self.g_lrT[0:16, tsl], ps[0:16, :], eng="dve")
            P.copy(self.g_lrT[32:48, tsl], ps[32:48, :], eng="dve")
        if getattr(self, "dbg_stop", 99) <= 1:
            return
        for h in range(4):
            self.gla_head(j, h, self.task_slot[tb + h])
            self.prefetch()

    def gla_head(self, j, h, slot):
        P = self.P
        PS = self.PS
        win = self.gw_in[slot]
        wout = self.gw_out[slot]
        qT, kT, vtok, oT = self.g_qT, self.g_kT, self.g_vtok, self.g_oT
        for t in range(NT):
            tsl = slice(t * 512, (t + 1) * 512)
            pq, pk = PS[0 + 2 * (t % 2)], PS[1 + 2 * (t % 2)]
            for c in range(8):
                P.matmul(pq[:], win[:, c, 0:128], self.hT[:, c, tsl], start=(c == 0), stop=(c == 7))
            for c in range(8):
                P.matmul(pk[:], win[:, c, 128:256], self.hT[:, c, tsl], start=(c == 0), stop=(c == 7))
            P.act(qT[:, tsl], pq[:], AF.Copy, scale=128.0 ** -0.5)
            P.copy(kT[:, tsl], pk[:], eng="dve")
        for b in range(16):
            pv = PS[4 + (b // 2) % 2]
            half = b % 2
            for c in range(8):
                P.matmul(pv[:, half * 256:(half + 1) * 256], self.hT[:, c, b * 128:(b + 1) * 128], win[:, c, 256:512],
                         start=(c == 0), stop=(c == 7))
            if half == 1:
                src = pv[:].re("p (a b) -> p a b", a=2)
                if (b // 2) % 2 == 0:
                    P.copy(vtok[:, b - 1:b + 1, :], src, eng="act")
                else:
                    P.copy(vtok[:, b - 1:b + 1, :], src, eng="dve")
        if getattr(self, "dbg_stop", 99) <= 2:
            return
        for step in range(getattr(self, "dbg_steps", 16)):
            for dr in range(getattr(self, "dbg_dirs", 2)):
                c = step if dr == 0 else 15 - step
                csl = slice(c * 128, (c + 1) * 128)
                pb = 0 if dr == 0 else 32
                U = self.tri[0] if dr == 0 else self.tri[1]
                Us = self.tri[2] if dr == 0 else self.tri[3]
                MK = self.tri[0] if dr == 0 else self.tri[2]
                bA, bB, bO, bK = PS[dr * 4], PS[dr * 4 + 1], PS[dr * 4 + 2], PS[dr * 4 + 3]
                e1, la, eq, ek, ekd = self.g_e1[dr], self.g_la[dr], self.g_eq[dr], self.g_ek[dr], self.g_ekd[dr]
                qt, ktl, kd, sm = self.g_qt[dr], self.g_kt[dr], self.g_kd[dr], self.g_sm[dr]
                S, Sbf = self.g_S[dr], self.g_Sbf[dr]
                P.matmul(bA[:, 0:128], self.g_lrT[pb:pb + 17, csl], self.g_w2[pb:pb + 17, h * 128:(h + 1) * 128])
                if getattr(self, "dbg_sub", 99) < 1:
                    continue
                P.act(e1[:], bA[:, 0:128], AF.Exp, scale=-1.0)
                P.act(la[:], e1[:], AF.Ln, bias=self.one_t[:])
                if getattr(self, "dbg_sub", 99) < 2:
                    continue
                P.matmul(bB[:, 0:128], la[:], U[:])
                P.matmul(bB[:, 128:256], Us[:], la[:])
                if getattr(self, "dbg_sub", 99) < 3:
                    continue
                P.transpose(bB[:, 256:384], kT[:, csl], self.ident[:])
                if getattr(self, "dbg_sub", 99) < 4:
                    continue
                nexp = getattr(self, "dbg_nexp", 3)
                if nexp >= 1:
                    P.act(eq[:], bB[:, 0:128], AF.Exp, scale=-1.0 / 16)
                if nexp >= 2:
                    P.act(ek[:], bB[:, 0:128], AF.Exp, scale=1.0 / 16)
                if nexp >= 3:
                    P.act(ekd[:], bB[:, 128:256], AF.Exp, scale=-1.0 / 16)
                if getattr(self, "dbg_sub", 99) < 5:
                    continue
                P.tt(qt[:], qT[:, csl], eq[:], ALU.mult)
                P.tt(ktl[:], kT[:, csl], ek[:], ALU.mult)
                P.tt(kd[:], bB[:, 256:384], ekd[:], ALU.mult)
                if getattr(self, "dbg_sub", 99) < 6:
                    continue
                P.matmul(bA[:, 128:256], ktl[:], qt[:])
                P.tt(sm[:], bA[:, 128:256], MK[:], ALU.mult)
                if getattr(self, "dbg_sub", 99) < 7:
                    continue
                for vc in range(2):
                    P.matmul(bO[:, vc * 128:(vc + 1) * 128], vtok[:, c, vc * 128:(vc + 1) * 128], sm[:],
                             start=True, stop=(step == 0))
                    if step > 0:
                        P.matmul(bO[:, vc * 128:(vc + 1) * 128], Sbf[:, vc * 128:(vc + 1) * 128], qt[:],
                                 start=False, stop=True)
                if getattr(self, "dbg_sub", 99) < 8:
                    continue
                osrc = bO[:, 0:256].re("p (a b) -> p a b", a=2)
                first = (c <= 7) if dr == 0 else (c >= 8)
                if first:
                    P.copy(oT[:, :, csl], osrc, eng="act")
                else:
                    P.tt(oT[:, :, csl], osrc, oT[:, :, csl], ALU.add)
                if getattr(self, "dbg_sub", 99) < 9:
                    continue
                if step < 15:
                    P.matmul(bK[:, 0:256], kd[:], vtok[:, c, :])
                    if step == 0:
                        P.copy(S[:], bK[:, 0:256], eng="dve")
                    else:
                        last = 127 if dr == 0 else 0
                        P.stt(S[:], S[:], eq[:, last:last + 1], bK[:, 0:256], ALU.mult, ALU.add)
                    P.copy(Sbf[:], S[:], eng="act")
        if getattr(self, "dbg_stop", 99) <= 3:
            return
        for t in range(NT):
            tsl = slice(t * 512, (t + 1) * 512)
            ss, rs = PS[6], PS[7]
            for vc in range(2):
                sq = self.sq[vc]
                P.act(sq[:], oT[:, vc, tsl], AF.Square)
                P.matmul(ss[:], self.ones_bf[:], sq[:], start=(vc == 0), stop=(vc == 1))
            lt = self.lnt[t % 2]
            P.act(lt[:], ss[:], AF.Ln, bias=self.eps_t[:], scale=1.0 / 256)
            P.act(rs[:], lt[:], AF.Exp, scale=-0.5)
            for vc in range(2):
                pg = PS[vc]
                for c in range(8):
                    P.matmul(pg[:], win[:, c, 512 + vc * 128:512 + (vc + 1) * 128], self.hT[:, c, tsl],
                             start=(c == 0), stop=(c == 7))
                P.act(self.g_sg[:], pg[:], AF.Silu)
                P.stt(oT[:, vc, tsl], oT[:, vc, tsl], self.gno[:, j * 2 + vc:j * 2 + vc + 1], rs[:], ALU.mult, ALU.mult)
                P.tt(self.g_onb[:, vc, :], oT[:, vc, tsl], self.g_sg[:], ALU.mult)
            for dc in range(8):
                pb_ = PS[2 + dc % 4]
                for vc in range(2):
                    P.matmul(pb_[:], wout[:, vc, dc * 128:(dc + 1) * 128], self.g_onb[:, vc, :], start=(vc == 0), stop=(vc == 1))
                P.tt(self.xT[:, dc, tsl], pb_[:], self.xT[:, dc, tsl], ALU.add)


    def diff_load(self, j, h, slot):
        P = self.P
        win = self.dw_in[slot]
        W = self.d["diff_w_in"][j].rearrange("(c p) n -> p c n", p=128)
        for q in range(3):
            P.dma(win[:, :, q * 128:(q + 1) * 128], W[:, :, q * 1024 + h * 128:q * 1024 + (h + 1) * 128], eng="pool")

    def diff_load_out(self, j, half, slot):
        P = self.P
        wo = self.d["diff_w_out"][j][half * 512:(half + 1) * 512, :].rearrange("(g p) d -> p g d", p=128)
        P.dma(self.dw_out[slot][:], wo, eng="pool")

    def diff_tasks(self, i):
        j = i // 2
        t = [((lambda slot, h=h: self.diff_load(j, h, slot)), False) for h in range(8)]
        t.append(((lambda slot: self.diff_load_out(j, 0, slot)), False))
        t.append(((lambda slot: self.diff_load_out(j, 1, slot)), False))
        return t

    def diff(self, i, tb):
        import math
        P = self.P
        PS = self.PS
        j = i // 2
        lam_init = 0.8 - 0.6 * math.exp(-0.3 * i)
        dstop = getattr(self, "dbg_dstop", 99)
        if dstop <= 0:
            return
        self.rmsnorm_all((i * 3 + 1) * 8)
        sm = self.d_small
        gq = self.d["diff_qk_norm_g"][j, 0].rearrange("(p o) -> p o", o=1)
        gk = self.d["diff_qk_norm_g"][j, 1].rearrange("(p o) -> p o", o=1)
        P.dma(sm[0:64, 0:1], gq, eng="sp")
        P.dma(sm[64:128, 0:1], gq, eng="sp")
        P.dma(sm[0:64, 1:2], gk, eng="sp")
        P.dma(sm[64:128, 1:2], gk, eng="sp")
        P.dma(sm[:, 2:3], self.d["diff_subln_g"][j].rearrange("(p o) -> p o", o=1), eng="sp")
        P.dma(self.d_lv[:], self.d["diff_lambda"][j].partition_broadcast(128), eng="sp")
        P.ts(sm[:, 3:4], sm[:, 0:1], 64.0 ** -0.5, None, ALU.mult)
        P.ts(sm[:, 4:5], sm[:, 2:3], 1.0 - lam_init, None, ALU.mult)
        lp = self.d_lp
        P.tt(lp[:, 0:64], self.d_lv[:, 0:64], self.d_lv[:, 64:128], ALU.mult)
        P.tt(lp[:, 64:128], self.d_lv[:, 128:192], self.d_lv[:, 192:256], ALU.mult)
        P.add("dve", lambda e: e.reduce_sum(sm[:, 5:6].ap, lp[:, 0:64].ap, mybir.AxisListType.X),
              reads=[lp[:, 0:64]], writes=[sm[:, 5:6]])
        P.add("dve", lambda e: e.reduce_sum(sm[:, 6:7].ap, lp[:, 64:128].ap, mybir.AxisListType.X),
              reads=[lp[:, 64:128]], writes=[sm[:, 6:7]])
        P.act(sm[:, 7:9], sm[:, 5:7], AF.Exp)
        P.tt(sm[:, 9:10], sm[:, 8:9], sm[:, 7:8], ALU.subtract)
        P.ts(sm[:, 10:11], sm[:, 9:10], -lam_init, None, ALU.add)
        gqs, gks, gsub, nlam = sm[:, 3:4], sm[:, 1:2], sm[:, 4:5], sm[:, 10:11]
        if dstop <= 1:
            return

        for h in range(8):
            win = self.dw_in[self.task_slot[tb + h]]
            for which, dst, gain in ((0, self.d_qn, gqs), (1, self.d_kn, gks)):
                for t in range(NT):
                    tsl = slice(t * 512, (t + 1) * 512)
                    pq = PS[(2 * t) % 4 + 4 * which]
                    ss = PS[(2 * t) % 4 + 1 + 4 * which]
                    for c in range(8):
                        P.matmul(pq[:], win[:, c, which * 128:(which + 1) * 128], self.hT[:, c, tsl],
                                 start=(c == 0), stop=(c == 7))
                    sq = self.sq[t % 2]
                    P.act(sq[:], pq[:], AF.Square)
                    P.matmul(ss[:], self.blk2[:], sq[:])
                    lt = self.lnt[t % 2]
                    P.act(lt[:], ss[:], AF.Ln, bias=self.eps_t[:], scale=1.0 / 64)
                    P.act(self.d_rs[:], lt[:], AF.Exp, scale=-0.5)
                    P.stt(dst[:, tsl], pq[:], gain, self.d_rs[:], ALU.mult, ALU.mult)
            if dstop <= 2:
                return
            for b in range(16):
                pv = PS[(b // 4) % 2]
                q4 = b % 4
                for c in range(8):
                    P.matmul(pv[:, q4 * 128:(q4 + 1) * 128], self.hT[:, c, b * 128:(b + 1) * 128], win[:, c, 256:384],
                             start=(c == 0), stop=(c == 7))
                if q4 == 3:
                    src = pv[:].re("p (a b) -> p a b", a=4)
                    P.copy(self.d_vtok[:, b - 3:b + 1, :], src, eng=("act" if (b // 4) % 2 == 0 else "dve"))
            if dstop <= 3:
                return
            src = bass.AP(self.ebscr, h * 128 * 1279 + 127, [[1278, 128], [1, 1152]])
            op = P.dma(self.d_eb[:], src, eng="sp")
            for w in self.eb_writes:
                if w not in op.deps:
                    op.deps.append(w)
                    w.needs_sig = True
            if dstop <= 4:
                P.copy(self.xT[:, 0, :], self.d_qn[:], eng="dve")
                P.copy(self.xT[:, 1, :], self.d_kn[:], eng="dve")
                P.copy(self.xT[:, 2, 0:1152], self.d_eb[:], eng="dve")
                P.copy(self.xT[:, 3, 0:16], sm[:], eng="dve")
                P.copy(self.xT[:, 3, 16:32], self.cfar[:], eng="dve")
                P.copy(self.xT[:, 4, :], self.d_vtok[:].re("p a b -> p (a b)"), eng="dve")
                return
            for qt in range(NT):
                self.diff_attn(h, qt, nlam, gsub)
                if dstop <= 5:
                    P.copy(self.xT[:, 0, 0:512], self.d_on[:, 0, 0:512], eng="dve")
                    P.copy(self.xT[:, 1, 0:512], self.d_t[2][:], eng="dve")
                    P.copy(self.xT[:, 2, 0:512], self.d_rz[0][:], eng="dve")
                    P.copy(self.xT[:, 3, 0:512], self.d_rz[1][:], eng="dve")
                    P.copy(self.xT[:, 4, 0:512], self.d_t[0][:], eng="dve")
                    P.copy(self.xT[:, 5, 0:512], self.d_pt[3][:], eng="dve")
                    P.copy(self.xT[:, 6, 0:512], self.d_pt[0][:], eng="dve")
                    return
            self.prefetch()
        wouts = [self.dw_out[self.task_slot[tb + 8]], self.dw_out[self.task_slot[tb + 9]]]
        for t in range(NT):
            tsl = slice(t * 512, (t + 1) * 512)
            for dc in range(8):
                pb = PS[dc % 4 + 4 * (t % 2)]
                for hh in range(8):
                    P.matmul(pb[:], wouts[hh // 4][:, hh % 4, dc * 128:(dc + 1) * 128], self.d_on[:, hh, tsl],
                             start=(hh == 0), stop=(hh == 7))
                P.tt(self.xT[:, dc, tsl], pb[:], self.xT[:, dc, tsl], ALU.add)
        self.prefetch()
        self.prefetch()

    def diff_attn(self, h, qt, nlam, gsub):
        P = self.P
        PS = self.PS
        qsl = slice(qt * 512, (qt + 1) * 512)
        qn, kn, vt = self.d_qn, self.d_kn, self.d_vtok
        O = [PS[4], PS[5]]
        Z = [PS[6], PS[7]]

        def S(kc):
            ksl = slice(kc * 128, (kc + 1) * 128)
            for comp in range(2):
                pb = comp * 64
                P.matmul(PS[(kc % 2) * 2 + comp][:], kn[pb:pb + 64, ksl], qn[pb:pb + 64, qsl])

        S(0)
        for kc in range(16):
            if kc + 1 < 16:
                S(kc + 1)
            delta = kc - 4 * qt
            for comp in range(2):
                sb = PS[(kc % 2) * 2 + comp]
                pt = self.d_pt[(kc % 2) * 2 + comp]
                if -1 <= delta <= 4:
                    e = self.d_e[(2 * kc + comp) % 3]
                    P.act(e[:], sb[:], AF.Exp)
                    off = 512 - 128 * delta
                    P.tt(pt[:], e[:], self.d_eb[:, off:off + 512], ALU.mult)
                else:
                    sgn = 0 if delta < 0 else 1
                    P.act(pt[:], sb[:], AF.Exp, bias=self.cfar[:, 2 * h + sgn:2 * h + sgn + 1])
            for comp in range(2):
                pt = self.d_pt[(kc % 2) * 2 + comp]
                P.matmul(O[comp][:], vt[:, kc, :], pt[:], start=(kc == 0), stop=(kc == 15))
                P.matmul(Z[comp][:], self.ones_bf[:], pt[:], start=(kc == 0), stop=(kc == 15))
        t0, t1, t2 = self.d_t
        for comp in range(2):
            lt = self.lnt[comp]
            P.act(lt[:], Z[comp][:], AF.Ln)
            P.act(self.d_rz[comp][:], lt[:], AF.Exp, scale=-1.0)
        P.tt(t0[:], O[0][:], self.d_rz[0][:], ALU.mult)
        P.stt(t1[:], O[1][:], nlam, self.d_rz[1][:], ALU.mult, ALU.mult)
        P.tt(t2[:], t0[:], t1[:], ALU.add)
        sq = self.sq[0]
        P.act(sq[:], t2[:], AF.Square)
        ss = PS[0]
        P.matmul(ss[:], self.ones_bf[:], sq[:])
        lt = self.lnt[0]
        P.act(lt[:], ss[:], AF.Ln, bias=self.eps_t[:], scale=1.0 / 128)
        P.act(self.d_rs[:], lt[:], AF.Exp, scale=-0.5)
        P.stt(self.d_on[:, h, qsl], t2[:], gsub, self.d_rs[:], ALU.mult, ALU.mult)

    def build(self):
        P = self.P
        P.memset(self.eps_t[:], EPS, eng="pool")
        P.memset(self.one_t[:], 1.0, eng="pool")
        self.eb_writes = []
        self.setup()
        if any(st[0] == "diff" for st in self.plan):
            self.setup_bias()
        bases = []
        for s in range(self.n_seq):
            for step in self.plan:
                bases.append(len(self.load_q))
                if step[0] == "ffn":
                    self.load_q += self.ffn_tasks(step[1], step[2])
                elif step[0] == "gla":
                    self.load_q += self.gla_tasks(step[1])
                elif step[0] == "diff":
                    self.load_q += self.diff_tasks(step[1])
                elif step[0] == "dbg_gate":
                    self.load_q += self.ffn_tasks(0, 0)[:1]
        self.pump()
        k = 0
        for s in range(self.n_seq):
            self.load_x(s)
            for step in self.plan:
                tb = bases[k]
                k += 1
                if step[0] == "ffn":
                    self.ffn(step[1], step[2], tb)
                elif step[0] == "gla":
                    self.in_mixer = True
                    self.gla(step[1], tb)
                    self.in_mixer = False
                    self.pump()
                elif step[0] == "diff":
                    self.in_mixer = True
                    self.diff(step[1], tb)
                    self.in_mixer = False
                    self.pump()
                elif step[0] == "dbg_gate":
                    self.rmsnorm_all(0)
                    wv = self.wv[self.task_slot[tb]]
                    tsl = slice(0, 512)
                    pg, pu = self.PS[0], self.PS[1]
                    for c in range(8):
                        P.matmul(pg[:], wv["gate"][:, c, 0:128], self.hT[:, c, tsl], start=(c == 0), stop=(c == 7))
                    for c in range(8):
                        P.matmul(pu[:], wv["up"][:, c, 0:128], self.hT[:, c, tsl], start=(c == 0), stop=(c == 7))
                    P.copy(self.xT[:, 0, tsl], pg[:], eng="dve")
                    P.copy(self.xT[:, 2, tsl], pu[:], eng="dve")
                    P.act(self.sg[0][:], pg[:], AF.Silu)
                    P.copy(self.xT[:, 1, tsl], self.sg[0][:], eng="dve")
                    P.tt(self.aT[0][:, 0, :], pu[:], self.sg[0][:], ALU.mult)
                    P.copy(self.xT[:, 3, tsl], self.aT[0][:, 0, :], eng="dve")
                    P.copy(self.xT[:, 4, tsl], wv["gate"][:, 0, 0:512], eng="dve")
                    P.copy(self.xT[:, 5, tsl], wv["down"][:, 0, 0:512], eng="dve")
                elif step[0] == "dbg_norm":
                    self.rmsnorm_all(step[1])
                    for c in range(8):
                        P.copy(self.xT[:, c, :], self.hT[:, c, :], eng="dve")
            self.store_x(s)
        return P.emit()


FULL_PLAN = []
for _i in range(DEPTH):
    FULL_PLAN += [("ffn", _i, 0), ("gla" if _i % 2 == 0 else "diff", _i), ("ffn", _i, 1)]

_CONSTS = None


def _t5_onehot():
    import math
    import jax
    import jax.numpy as jnp
    cpu = jax.devices("cpu")[0]
    with jax.default_device(cpu):
        u = jnp.arange(1279, dtype=jnp.int32)
        rel = 639 - u
        nb = 16
        max_exact = 8
        ret = (rel > 0).astype(jnp.int32) * nb
        n = jnp.abs(rel)
        nf = jnp.maximum(n, 1).astype(jnp.float32)
        large = max_exact + (jnp.log(nf / max_exact) / math.log(128 / max_exact) * (nb - max_exact)).astype(jnp.int32)
        large = jnp.minimum(large, nb - 1)
        bucket = np.asarray(ret + jnp.where(n < max_exact, n, large))
    oh = np.zeros((32, 1281), np.float32)
    oh[bucket, np.arange(1279)] = 1.0
    oh[15, 1279] = 1.0
    oh[31, 1280] = 1.0
    return oh


def _consts():
    global _CONSTS
    if _CONSTS is None:
        c = {}
        c["c_ident"] = np.eye(128, dtype=np.float32)
        r = np.arange(128)[:, None]
        q = np.arange(128)[None, :]
        c["c_tri"] = np.stack([(r <= q), (r >= q), (r > q), (r < q)]).astype(np.float32)
        c["c_oh"] = _t5_onehot()
        _CONSTS = c
    return _CONSTS


def run_plan(inputs, plan, n_seq=2, n_cores=8, trace=False, decl_depth=DEPTH, **dbg):
    nc = bass.Bass("TRN2", target_bir_lowering=False)
    k = K(nc, n_seq, plan, decl_depth, **dbg)
    stats = k.build()
    print("program stats:", stats, flush=True)
    x = np.ascontiguousarray(np.asarray(inputs["x"], dtype=np.float32))
    shared = {}
    for name in k.d:
        if name == "x" or name.startswith("c_"):
            continue
        a = np.ascontiguousarray(np.asarray(inputs[name], dtype=np.float32))
        shp = tuple(k.d[name].shape)
        if a.size != int(np.prod(shp)):
            a = a[:shp[0]]
        shared[name] = np.ascontiguousarray(a).reshape(shp)
    shared.update({n: v for n, v in _consts().items() if n in k.d})
    in_maps = []
    for c in range(n_cores):
        m = dict(shared)
        m["x"] = x[c * n_seq:(c + 1) * n_seq]
        in_maps.append(m)
    res = run_bass_kernel_spmd(nc, in_maps, core_ids=list(range(n_cores)), trace=trace)
    out = np.concatenate([r["y"] for r in res.results], axis=0)
    return out, res


def kernel(**inputs):
    out, _ = run_plan(inputs, FULL_PLAN, n_seq=2, n_cores=8)
    return out.astype(np.float32)
```

```python
import numpy as np
import concourse.bass as bass
import concourse.mybir as mybir

F32 = mybir.dt.float32
BF16 = mybir.dt.bfloat16
ALU = mybir.AluOpType
AF = mybir.ActivationFunctionType
DTSIZE = {F32: 4, BF16: 2}

CELL = 128
SEM_LIMIT = 12000
N_DMA_SEMS = 24
SAME_ENGINE_SYNC = True
SB_BASE = 16512
SB_END = 229344


class Acc:
    __slots__ = ("ap", "rng")

    def __init__(self, ap, rng):
        self.ap = ap
        self.rng = rng

    def re(self, pattern, **kw):
        return Acc(self.ap.rearrange(pattern, **kw), self.rng)


class Buf:
    def __init__(self, prog, name, shape, dtype, space, base, handle):
        self.prog = prog
        self.name = name
        self.shape = list(shape)
        self.dtype = dtype
        self.esz = DTSIZE[dtype]
        self.space = space
        self.base = base
        self.t = handle
        fs = self.shape[1:]
        st = [1] * len(fs)
        for i in range(len(fs) - 2, -1, -1):
            st[i] = st[i + 1] * fs[i + 1]
        self.strides = st

    def __getitem__(self, idx):
        if not isinstance(idx, tuple):
            idx = (idx,)
        idx = list(idx) + [slice(None)] * (len(self.shape) - len(idx))
        ap = self.t[tuple(idx)]
        fidx = idx[1:]
        fs = self.shape[1:]
        lohi = []
        for k, ix in enumerate(fidx):
            if isinstance(ix, slice):
                lo = 0 if ix.start is None else ix.start
                hi = fs[k] if ix.stop is None else ix.stop
                assert ix.step in (None, 1)
            else:
                lo, hi = ix, ix + 1
            assert 0 <= lo < hi <= fs[k], (self.name, idx, self.shape)
            lohi.append((lo, hi))
        n = len(fs)
        k = n - 1
        run = 1
        while k >= 0 and lohi[k] == (0, fs[k]):
            run *= fs[k]
            k -= 1
        ranges = []
        if k < 0:
            ranges.append((0, run))
        else:
            outer = lohi[:k]
            lo_k, hi_k = lohi[k]
            seg = (hi_k - lo_k) * self.strides[k]

            def rec(d, off):
                if d == k:
                    s = off + lo_k * self.strides[k]
                    ranges.append((s, s + seg))
                    return
                for i in range(outer[d][0], outer[d][1]):
                    rec(d + 1, off + i * self.strides[d])
            rec(0, 0)
        if self.space == "ps":
            b = self.base // 2048
            return Acc(ap, [("ps", b, b + 1)])
        rng = []
        for (s, e) in ranges:
            b0 = self.base + s * self.esz
            b1 = self.base + e * self.esz
            rng.append((self.space, b0 // CELL, (b1 - 1) // CELL + 1))
        return Acc(ap, rng)


class Op:
    __slots__ = ("eng", "fn", "deps", "is_dma", "sig", "needs_sig", "idx", "tag")

    def __init__(self, eng, fn, is_dma, tag):
        self.eng = eng
        self.fn = fn
        self.deps = []
        self.is_dma = is_dma
        self.sig = None
        self.needs_sig = False
        self.tag = tag


class Prog:
    ENGS = ("pe", "act", "dve", "pool", "sp")

    def __init__(self, nc):
        self.nc = nc
        self.ops = {e: [] for e in self.ENGS}
        self.nops = 0
        self.sb_off = SB_BASE
        self.ps_bank = 0
        self.lastw = {"sb": {}, "ps": {}}
        self.readers = {"sb": {}, "ps": {}}
        self.sems = []
        self.final_waits = []
        self.n_bufs = 0

    def sbuf(self, name, shape, dtype, at=None, align=CELL):
        esz = DTSIZE[dtype]
        nbytes = int(np.prod(shape[1:])) * esz
        if at is None:
            off = (self.sb_off + align - 1) // align * align
            self.sb_off = off + nbytes
            assert self.sb_off <= SB_END, (name, self.sb_off)
        else:
            off = at
        self.n_bufs += 1
        h = self.nc.alloc_sbuf_tensor_at(f"{name}_{self.n_bufs}", list(shape), dtype, offset=off)
        return Buf(self, name, shape, dtype, "sb", off, h)

    def psum_bank(self, name):
        b = self.ps_bank
        self.ps_bank += 1
        assert b < 8
        h = self.nc.alloc_psum_tensor(f"{name}_{b}", [128, 512], F32)
        return Buf(self, name, [128, 512], F32, "ps", b * 2048, h)

    def add(self, eng, fn, reads=(), writes=(), is_dma=False, tag=None):
        op = Op(eng, fn, is_dma, tag)
        op.idx = self.nops
        self.nops += 1
        deps = set()
        ps_reads = [a for a in reads if a.rng and a.rng[0][0] == "ps"]
        if ps_reads:
            reads = [a for a in reads if not (a.rng and a.rng[0][0] == "ps")]
            writes = list(writes) + ps_reads
        for a in reads:
            for (sp, c0, c1) in a.rng:
                lw = self.lastw[sp]
                for c in range(c0, c1):
                    w = lw.get(c)
                    if w is not None:
                        deps.add(w)
        for a in writes:
            for (sp, c0, c1) in a.rng:
                lw = self.lastw[sp]
                rd = self.readers[sp]
                for c in range(c0, c1):
                    w = lw.get(c)
                    if w is not None:
                        deps.add(w)
                    r = rd.get(c)
                    if r:
                        deps.update(r)
        for a in reads:
            for (sp, c0, c1) in a.rng:
                rd = self.readers[sp]
                for c in range(c0, c1):
                    l = rd.get(c)
                    if l is None:
                        rd[c] = [op]
                    elif l[-1] is not op:
                        l.append(op)
        for a in writes:
            for (sp, c0, c1) in a.rng:
                lw = self.lastw[sp]
                rd = self.readers[sp]
                for c in range(c0, c1):
                    lw[c] = op
                    if c in rd:
                        rd[c] = None
        deps.discard(op)
        for d in deps:
            if d.eng == eng and not d.is_dma:
                if eng == "pe" or not SAME_ENGINE_SYNC:
                    continue
            op.deps.append(d)
            d.needs_sig = True
        self.ops[eng].append(op)
        return op

    def matmul(self, out, lhsT, rhs, start=True, stop=True, **kw):
        return self.add("pe", lambda e: e.matmul(out.ap, lhsT.ap, rhs.ap, start=start, stop=stop, **kw),
                        reads=[lhsT, rhs], writes=[out])

    def transpose(self, out, in_, ident):
        return self.add("pe", lambda e: e.transpose(out.ap, in_.ap, ident.ap),
                        reads=[in_, ident], writes=[out])

    def act(self, out, in_, func, bias=None, scale=1.0, accum_out=None, eng="act"):
        reads = [in_]
        kw = {}
        if bias is not None:
            if isinstance(bias, Acc):
                reads.append(bias)
                kw["bias"] = bias.ap
            else:
                kw["bias"] = bias
        if isinstance(scale, Acc):
            reads.append(scale)
            kw["scale"] = scale.ap
        else:
            kw["scale"] = scale
        writes = [out]
        if accum_out is not None:
            writes.append(accum_out)
            kw["accum_out"] = accum_out.ap
        return self.add(eng, lambda e: e.activation(out.ap, in_.ap, func, **kw), reads=reads, writes=writes)

    def tt(self, out, in0, in1, op, eng="dve"):
        return self.add(eng, lambda e: e.tensor_tensor(out.ap, in0.ap, in1.ap, op), reads=[in0, in1], writes=[out])

    def ts(self, out, in0, s1, s2, op0, op1=None, eng="dve"):
        reads = [in0]
        a1 = s1.ap if isinstance(s1, Acc) else s1
        a2 = s2.ap if isinstance(s2, Acc) else s2
        if isinstance(s1, Acc):
            reads.append(s1)
        if isinstance(s2, Acc):
            reads.append(s2)
        if op1 is None:
            return self.add(eng, lambda e: e.tensor_scalar(out.ap, in0.ap, a1, None, op0), reads=reads, writes=[out])
        return self.add(eng, lambda e: e.tensor_scalar(out.ap, in0.ap, a1, a2, op0, op1), reads=reads, writes=[out])

    def stt(self, out, in0, scalar, in1, op0, op1, eng="dve"):
        reads = [in0, in1]
        sc = scalar.ap if isinstance(scalar, Acc) else scalar
        if isinstance(scalar, Acc):
            reads.append(scalar)
        return self.add(eng, lambda e: e.scalar_tensor_tensor(out.ap, in0.ap, sc, in1.ap, op0, op1),
                        reads=reads, writes=[out])

    def copy(self, out, in_, eng="dve"):
        if eng == "act":
            return self.add("act", lambda e: e.copy(out.ap, in_.ap), reads=[in_], writes=[out])
        return self.add(eng, lambda e: e.tensor_copy(out.ap, in_.ap), reads=[in_], writes=[out])

    def recip(self, out, in_):
        return self.add("dve", lambda e: e.reciprocal(out.ap, in_.ap), reads=[in_], writes=[out])

    def memset(self, out, val, eng="pool"):
        return self.add(eng, lambda e: e.memset(out.ap, val), reads=[], writes=[out])

    def dma(self, out, in_, eng="sp", final=False, **kw):
        reads = [in_] if isinstance(in_, Acc) else []
        writes = [out] if isinstance(out, Acc) else []
        oap = out.ap if isinstance(out, Acc) else out
        iap = in_.ap if isinstance(in_, Acc) else in_
        op = self.add(eng, lambda e: e.dma_start(out=oap, in_=iap, **kw), reads=reads, writes=writes, is_dma=True)
        if final:
            op.needs_sig = True
            self.final_waits.append(op)
        return op

    def _new_sem(self, name):
        cm = self.nc.semaphore(f"{name}_{len(self.sems)}")
        h = cm.__enter__()
        self.sems.append(cm)
        return h

    def emit(self):
        nc = self.nc
        dma_sems = [[self._new_sem("dma"), 0, None] for _ in range(N_DMA_SEMS)]
        dma_rr = 0
        all_dma = sorted([op for e in self.ENGS for op in self.ops[e] if op.is_dma], key=lambda o: o.idx)
        for op in all_dma:
            slot = dma_sems[dma_rr % N_DMA_SEMS]
            dma_rr += 1
            if slot[1] + 16 > SEM_LIMIT:
                slot[0] = self._new_sem("dma")
                slot[1] = 0
                slot[2] = None
            if slot[2] is not None:
                prev = slot[2]
                if prev not in op.deps:
                    op.deps.append(prev)
            slot[1] += 16
            op.sig = (slot[0], slot[1])
            slot[2] = op
        for e in self.ENGS:
            sem = None
            cnt = 0
            for op in self.ops[e]:
                if op.is_dma or not op.needs_sig:
                    continue
                if sem is None or cnt + 1 > SEM_LIMIT:
                    sem = self._new_sem(e)
                    cnt = 0
                cnt += 1
                op.sig = (sem, cnt)
        nwaits = {e: 0 for e in self.ENGS}

        def emit_engine(ename, eng):
            seen = {}
            for op in self.ops[ename]:
                need = {}
                for d in op.deps:
                    s, v = d.sig
                    if seen.get(s, 0) >= v:
                        continue
                    if need.get(s, (None, 0))[1] < v:
                        need[s] = (s, v)
                for (s, v) in need.values():
                    eng.wait_ge(s, v)
                    seen[s] = v
                    nwaits[ename] += 1
                ins = op.fn(eng)
                if op.sig is not None:
                    ins.then_inc(op.sig[0], 16 if op.is_dma else 1)
            if ename == "sp":
                for op in self.final_waits:
                    s, v = op.sig
                    if seen.get(s, 0) < v:
                        eng.wait_ge(s, v)
                        seen[s] = v

        with nc.Block() as block:
            @block.tensor
            def _(e):
                emit_engine("pe", e)

            @block.scalar
            def _(e):
                emit_engine("act", e)

            @block.vector
            def _(e):
                emit_engine("dve", e)

            @block.gpsimd
            def _(e):
                emit_engine("pool", e)

            @block.sync
            def _(e):
                emit_engine("sp", e)
        self.stats = {e: (len(self.ops[e]), nwaits[e]) for e in self.ENGS}
        self.stats["sems"] = len(self.sems)
        return self.stats

from concourse.bass_utils import run_bass_kernel_spmd

D_MODEL = 1024
SEQ = 2048
DEPTH = 4
D_FF = 2816
NF = D_FF // 128
EPS = 1e-6
FFN_GROUPS = [(0, 4), (4, 4), (8, 4), (12, 4), (16, 3), (19, 3)]
NT = SEQ // 512


class K:
    def __init__(self, nc, n_seq, plan, decl_depth=DEPTH, **dbg):
        self.nc = nc
        for k_, v_ in dbg.items():
            setattr(self, k_, v_)
        self.n_seq = n_seq
        self.plan = plan
        P = self.P = Prog(nc)
        d = self.d = {}

        def din(name, shape):
            d[name] = nc.dram_tensor(name, list(shape), F32, kind="ExternalInput").ap()
        din("x", [n_seq, SEQ, D_MODEL])
        din("norm_g", [DEPTH * 3 * 8, 128])
        din("ffn_w_gu", [decl_depth, 2, D_MODEL, 2 * D_FF])
        din("ffn_w_down", [decl_depth, 2, D_FF, D_MODEL])
        din("c_ident", [128, 128])
        din("c_tri", [4, 128, 128])
        din("gla_w_in", [max(1, decl_depth // 2), D_MODEL, 3104])
        din("gla_w_gate2", [max(1, decl_depth // 2), 2, 16, 512])
        din("gla_b_gate", [max(1, decl_depth // 2), 2, 512])
        din("gla_o_norm_g", [4, 128])
        din("gla_w_out", [max(1, decl_depth // 2), 1024, 1024])
        din("c_oh", [32, 1281])
        din("diff_w_in", [max(1, decl_depth // 2), D_MODEL, 3072])
        din("diff_qk_norm_g", [max(1, decl_depth // 2), 2, 64])
        din("diff_lambda", [max(1, decl_depth // 2), 256])
        din("diff_subln_g", [max(1, decl_depth // 2), 128])
        din("diff_w_out", [max(1, decl_depth // 2), 1024, 1024])
        din("rel_bias_table", [32, 8])
        self.ebscr = nc.dram_tensor("ebscr", [8 * 128 * 1279], F32)
        self.y = nc.dram_tensor("y", [n_seq, SEQ, D_MODEL], F32, kind="ExternalOutput").ap()

        self.xT = P.sbuf("xT", [128, 8, SEQ], F32)
        self.hT = P.sbuf("hT", [128, 8, SEQ], BF16)
        self.ident = P.sbuf("ident", [128, 128], F32)
        self.ones_bf = P.sbuf("ones_bf", [128, 128], BF16)
        self.gn = P.sbuf("gn", [128, DEPTH * 3 * 8], F32)
        self.tri = [P.sbuf(f"tri{i}", [128, 128], F32) for i in range(4)]
        self.eps_t = P.sbuf("eps_t", [128, 1], F32)
        self.blk2 = P.sbuf("blk2", [128, 128], BF16)
        self.cfar = P.sbuf("cfar", [128, 16], F32)
        self.one_t = P.sbuf("one_t", [128, 1], F32)
        self.gno = P.sbuf("gno", [128, 4], F32)
        self.wslot_off = []
        for i in range(2):
            b = P.sbuf(f"wslot{i}", [128, 12288], BF16)
            self.wslot_off.append(b.base)
        self.wv = []
        for i in range(2):
            o = self.wslot_off[i]
            self.wv.append(dict(
                gate=P.sbuf(f"wg{i}", [128, 8, 512], BF16, at=o),
                up=P.sbuf(f"wu{i}", [128, 8, 512], BF16, at=o + 8192),
                down=P.sbuf(f"wd{i}", [128, 4, 1024], BF16, at=o + 16384),
            ))
        self.gw_in = [P.sbuf(f"gwin{i}", [128, 8, 768], BF16, at=self.wslot_off[i]) for i in range(2)]
        self.gw_out = [P.sbuf(f"gwout{i}", [128, 2, 1024], BF16, at=self.wslot_off[i] + 12288) for i in range(2)]
        self.sq = [P.sbuf(f"sq{i}", [128, 512], BF16) for i in range(2)]
        self.lnt = [P.sbuf(f"lnt{i}", [128, 512], F32) for i in range(2)]
        arena = P.sb_off
        print("arena base", arena, "bytes avail", SB_END - arena, flush=True)
        self.stage = [P.sbuf(f"stage{i}", [128, 1024], F32) for i in range(2)]
        P.sb_off = arena
        self.aT = [P.sbuf(f"aT{i}", [128, 4, 512], BF16) for i in range(2)]
        self.sg = [P.sbuf(f"sg{i}", [128, 512], F32) for i in range(2)]
        P.sb_off = arena
        self.g_vtok = P.sbuf("g_vtok", [128, 16, 256], BF16)
        self.g_oT = P.sbuf("g_oT", [128, 2, SEQ], F32)
        self.g_lrT = P.sbuf("g_lrT", [64, SEQ], BF16)
        self.g_onb = P.sbuf("g_onb", [128, 2, 512], BF16)
        self.g_qtl = P.sbuf("g_qtl", [128, 2, SEQ], BF16)
        self.g_kdc = P.sbuf("g_kdc", [128, 16, 2, 128], BF16)
        self.g_qk = P.sbuf("g_qk", [128, 2, 512], F32)
        self.g_eend = P.sbuf("g_eend", [128, 32], F32)
        self.g_sg = P.sbuf("g_sg", [128, 512], F32, at=self.g_qk.base)
        gla_end = P.sb_off
        P.sb_off = self.wslot_off[0] + 16384
        self.g_la = [P.sbuf(f"g_la{i}", [128, 2, 128], F32) for i in range(2)]
        self.g_eq = [P.sbuf(f"g_eq{i}", [128, 2, 128], F32) for i in range(2)]
        self.g_ek = [P.sbuf(f"g_ek{i}", [128, 2, 128], F32) for i in range(2)]
        self.g_w2 = P.sbuf("g_w2", [64, 512], BF16)
        self.g_wlr = P.sbuf("g_wlr", [128, 8, 64], BF16)
        assert P.sb_off <= self.wslot_off[0] + 24576, P.sb_off - self.wslot_off[0]
        P.sb_off = self.wslot_off[1] + 16384
        self.g_ekd = [P.sbuf(f"g_ekd{i}", [128, 2, 128], F32) for i in range(2)]
        self.g_kt = [P.sbuf(f"g_kt{i}", [128, 2, 128], BF16) for i in range(2)]
        self.g_sm = [P.sbuf(f"g_sm{i}", [128, 2, 128], BF16) for i in range(2)]
        self.g_S = [P.sbuf(f"g_S{i}", [128, 256], F32) for i in range(2)]
        self.g_Sbf = [[P.sbuf(f"g_Sbf{i}{k}", [128, 256], BF16) for k in range(2)] for i in range(2)]
        assert P.sb_off <= self.wslot_off[1] + 24576, P.sb_off - self.wslot_off[1]
        P.sb_off = gla_end
        print("gla arena end", P.sb_off, flush=True)
        P.sb_off = arena
        self.d_on = P.sbuf("d_on", [128, 8, SEQ], BF16)
        self.d_qz = [P.sbuf(f"d_qz{i}", [128, SEQ], BF16) for i in range(2)]
        self.d_kn = P.sbuf("d_kn", [128, SEQ], BF16)
        self.d_vtok = P.sbuf("d_vtok", [128, 16, 128], BF16)
        print("diff arena end", P.sb_off, flush=True)
        dend = P.sb_off
        P.sb_off = self.wslot_off[0] + 8192
        self.d_e = [P.sbuf(f"d_e{i}", [128, 512], F32) for i in range(3)]
        self.d_pt = [P.sbuf(f"d_pt{i}", [128, 512], BF16) for i in range(4)]
        self.d_t = [P.sbuf(f"d_t{i}", [128, 512], F32) for i in range(3)]
        assert P.sb_off <= self.wslot_off[0] + 24576
        P.sb_off = self.wslot_off[1] + 8192
        self.d_rs = P.sbuf("d_rs", [128, 512], F32)
        self.d_eb = P.sbuf("d_eb", [128, 1152], F32)
        self.d_rz = [P.sbuf(f"d_rz{i}", [128, 512], F32) for i in range(2)]
        self.d_small = P.sbuf("d_small", [128, 16], F32)
        self.d_lv = P.sbuf("d_lv", [128, 256], F32)
        self.d_lp = P.sbuf("d_lp", [128, 128], F32)
        assert P.sb_off <= self.wslot_off[1] + 24576
        P.sb_off = max(dend, gla_end)
        self.dw_in = [P.sbuf(f"dwin{i}", [128, 8, 384], BF16, at=self.wslot_off[i]) for i in range(2)]
        self.dw_out = [P.sbuf(f"dwout{i}", [128, 4, 1024], BF16, at=self.wslot_off[i]) for i in range(2)]
        self.PS = [P.psum_bank("ps") for _ in range(8)]
        self.load_q = []
        self.load_issued = 0
        self.tasks_done = 0
        self.in_mixer = False
        self.task_slot = {}

    def prefetch(self):
        self.tasks_done += 1
        self.pump()

    def pump(self):
        while self.load_issued < min(len(self.load_q), self.tasks_done + 2):
            k = self.load_issued
            fn, full = self.load_q[k]
            if full and self.in_mixer:
                return
            self.load_issued += 1
            self.task_slot[k] = k % 2
            fn(k % 2)

    def setup(self):
        P = self.P
        P.dma(self.ident[:], self.d["c_ident"], eng="sp")
        P.memset(self.ones_bf[:], 1.0, eng="pool")
        st = self.stage[0]
        P.dma(st[0:96, 0:128], self.d["norm_g"], eng="sp")
        ps = self.PS[0]
        P.transpose(ps[:, 0:96], st[0:96, 0:128], self.ident[0:96, 0:96])
        P.copy(self.gn[:], ps[:, 0:96], eng="dve")
        for q in range(4):
            P.dma(self.tri[q][:], self.d["c_tri"][q], eng="sp")
        st1 = self.stage[1]
        P.dma(st1[0:4, 0:128], self.d["gla_o_norm_g"], eng="sp")
        ps1 = self.PS[1]
        P.transpose(ps1[:, 0:4], st1[0:4, 0:128], self.ident[0:4, 0:4])
        P.copy(self.gno[:], ps1[:, 0:4], eng="dve")


    def setup_bias(self):
        P = self.P
        PS = self.PS
        st = self.stage
        P.memset(self.blk2[:], 0.0, eng="pool")
        P.memset(self.blk2[0:64, 0:64], 1.0, eng="pool")
        P.memset(self.blk2[64:128, 64:128], 1.0, eng="pool")
        oh = st[0]
        oh2 = st[1]
        P.dma(oh[0:32, 0:1024], self.d["c_oh"][:, 0:1024], eng="sp")
        P.dma(oh2[0:32, 0:257], self.d["c_oh"][:, 1024:1281], eng="sp")
        P.dma(oh2[0:32, 512:520], self.d["rel_bias_table"], eng="sp")
        P.memset(oh2[0:32, 640:768], 1.0, eng="dve")
        sbs = getattr(self, "dbg_sbs", 99)
        if sbs <= 0:
            return
        for h in range(8):
            tb = oh2[0:32, 768:896]
            P.ts(tb, oh2[0:32, 640:768], oh2[0:32, 512 + h:513 + h], None, ALU.mult)
            b0, b1, b2 = PS[(3 * h) % 8], PS[(3 * h + 1) % 8], PS[(3 * h + 2) % 8]
            if sbs <= 1:
                continue
            P.matmul(b0[:, 0:512], tb, oh[0:32, 0:512])
            P.matmul(b1[:, 0:512], tb, oh[0:32, 512:1024])
            P.matmul(b2[:, 0:257], tb, oh2[0:32, 0:257])
            if sbs <= 2:
                continue
            e = self.g_oT
            P.act(e[:, 0, 0:512], b0[:, 0:512], AF.Exp)
            P.act(e[:, 0, 512:1024], b1[:, 0:512], AF.Exp)
            P.act(e[:, 0, 1024:1279], b2[:, 0:255], AF.Exp)
            P.copy(self.cfar[:, 2 * h:2 * h + 2], b2[:, 255:257], eng="dve")
            if sbs <= 3:
                continue
            dst = bass.AP(self.ebscr, h * 128 * 1279, [[1279, 128], [1, 1279]])
            op = P.dma(dst, e[:, 0, 0:1279], eng="sp")
            self.eb_writes.append(op)

    def load_x(self, s):
        P = self.P
        for b in range(SEQ // 128):
            st = self.stage[b % 2]
            P.dma(st[:], self.d["x"][s, b * 128:(b + 1) * 128, :], eng="sp")
            for half in range(2):
                ps = self.PS[(2 * b + half) % 4]
                for cc in range(4):
                    c = half * 4 + cc
                    P.transpose(ps[:, cc * 128:(cc + 1) * 128], st[:, c * 128:(c + 1) * 128], self.ident[:])
                dst = self.xT[:, half * 4:half * 4 + 4, b * 128:(b + 1) * 128]
                src = ps[:].re("p (a b) -> p a b", a=4)
                if half == 0:
                    P.copy(dst, src, eng="dve")
                else:
                    P.copy(dst, src, eng="act")

    def store_x(self, s):
        P = self.P
        for b in range(SEQ // 128):
            st = self.stage[b % 2]
            for half in range(2):
                ps = self.PS[(2 * b + half) % 4]
                for cc in range(4):
                    c = half * 4 + cc
                    P.transpose(ps[:, cc * 128:(cc + 1) * 128], self.xT[:, c, b * 128:(b + 1) * 128], self.ident[:])
                dst = st[:, half * 512:(half + 1) * 512]
                if half == 0:
                    P.copy(dst, ps[:], eng="dve")
                else:
                    P.copy(dst, ps[:], eng="act")
            P.dma(self.y[s, b * 128:(b + 1) * 128, :], st[:], eng="sp", final=True)

    def rmsnorm_all(self, gcol):
        P = self.P
        for t in range(NT):
            tsl = slice(t * 512, (t + 1) * 512)
            ss = self.PS[4 + (t % 2) * 2]
            rs = self.PS[5 + (t % 2) * 2]
            for c in range(8):
                sq = self.sq[c % 2]
                P.act(sq[:], self.xT[:, c, tsl], AF.Square)
                P.matmul(ss[:], self.ones_bf[:], sq[:], start=(c == 0), stop=(c == 7))
            lt = self.lnt[t % 2]
            P.act(lt[:], ss[:], AF.Ln, bias=self.eps_t[:], scale=1.0 / D_MODEL)
            P.act(rs[:], lt[:], AF.Exp, scale=-0.5)
            for c in range(8):
                P.stt(self.hT[:, c, tsl], self.xT[:, c, tsl], self.gn[:, gcol + c:gcol + c + 1], rs[:],
                      ALU.mult, ALU.mult)

    def ffn_load(self, i, j, gi, slot):
        P = self.P
        f0, G = FFN_GROUPS[gi]
        wv = self.wv[slot]
        wgu = self.d["ffn_w_gu"][i, j].rearrange("(c p) n -> p c n", p=128)
        P.dma(wv["gate"][:, :, 0:G * 128], wgu[:, :, f0 * 128:(f0 + G) * 128], eng="pool")
        P.dma(wv["up"][:, :, 0:G * 128], wgu[:, :, D_FF + f0 * 128:D_FF + (f0 + G) * 128], eng="pool")
        wd = self.d["ffn_w_down"][i, j][f0 * 128:(f0 + G) * 128, :].rearrange("(g p) d -> p g d", p=128)
        P.dma(wv["down"][:, 0:G, :], wd, eng="pool")

    def ffn_tasks(self, i, j):
        return [((lambda slot, gi=gi: self.ffn_load(i, j, gi, slot)), True) for gi in range(len(FFN_GROUPS))]

    def ffn(self, i, j, tbase):
        P = self.P
        gcol = (i * 3 + (0 if j == 0 else 2)) * 8
        self.rmsnorm_all(gcol)
        ngroups = len(FFN_GROUPS)
        slots = self.task_slot
        units = [(gi, t, fi) for gi in range(ngroups) for t in range(NT) for fi in range(FFN_GROUPS[gi][1])]
        ucount = [0]
        dcount = [0]

        def U(n):
            gi, t, fi = units[n]
            wv = self.wv[slots[tbase + gi]]
            tsl = slice(t * 512, (t + 1) * 512)
            pg = self.PS[(n % 2) * 2]
            pu = self.PS[(n % 2) * 2 + 1]
            for c in range(8):
                P.matmul(pg[:], wv["gate"][:, c, fi * 128:(fi + 1) * 128], self.hT[:, c, tsl], start=(c == 0), stop=(c == 7))
            for c in range(8):
                P.matmul(pu[:], wv["up"][:, c, fi * 128:(fi + 1) * 128], self.hT[:, c, tsl], start=(c == 0), stop=(c == 7))

        def E(n):
            gi, t, fi = units[n]
            G = FFN_GROUPS[gi][1]
            pg = self.PS[(n % 2) * 2]
            pu = self.PS[(n % 2) * 2 + 1]
            sg = self.sg[n % 2]
            a = self.aT[(gi * NT + t) % 2]
            P.act(sg[:], pg[:], AF.Silu)
            P.tt(a[:, fi, :], pu[:], sg[:], ALU.mult)
            if fi == G - 1:
                wv = self.wv[slots[tbase + gi]]
                tsl = slice(t * 512, (t + 1) * 512)
                for dc in range(8):
                    bank = self.PS[4 + dcount[0] % 4]
                    dcount[0] += 1
                    for f in range(G):
                        P.matmul(bank[:], wv["down"][:, f, dc * 128:(dc + 1) * 128], a[:, f, :], start=(f == 0), stop=(f == G - 1))
                    P.stt(self.xT[:, dc, tsl], bank[:], 0.5, self.xT[:, dc, tsl], ALU.mult, ALU.add)
                if t == NT - 1:
                    self.prefetch()

        N = len(units)
        U(0)
        for n in range(N):
            if n + 1 < N:
                U(n + 1)
            E(n)


    def gla_load(self, j, h, slot):
        P = self.P
        win = self.gw_in[slot]
        wout = self.gw_out[slot]
        W = self.d["gla_w_in"][j].rearrange("(c p) n -> p c n", p=128)
        P.dma(win[:, :, 0:128], W[:, :, h * 128:(h + 1) * 128], eng="pool")
        P.dma(win[:, :, 128:256], W[:, :, 512 + h * 128:512 + (h + 1) * 128], eng="pool")
        P.dma(win[:, :, 256:512], W[:, :, 1024 + h * 256:1024 + (h + 1) * 256], eng="pool")
        P.dma(win[:, :, 512:768], W[:, :, 2048 + h * 256:2048 + (h + 1) * 256], eng="pool")
        wo = self.d["gla_w_out"][j][h * 256:(h + 1) * 256, :].rearrange("(g p) d -> p g d", p=128)
        P.dma(wout[:], wo, eng="pool")

    def gla_tasks(self, i):
        j = i // 2
        return [((lambda slot, h=h: self.gla_load(j, h, slot)), False) for h in range(4)]

    def gla(self, i, tb):
        P = self.P
        j = i // 2
        PS = self.PS
        self.rmsnorm_all((i * 3 + 1) * 8)
        W = self.d["gla_w_in"][j].rearrange("(c p) n -> p c n", p=128)
        P.memset(self.g_wlr[:], 0.0, eng="pool")
        P.dma(self.g_wlr[:, :, 0:16], W[:, :, 3072:3088], eng="pool")
        P.dma(self.g_wlr[:, :, 32:48], W[:, :, 3088:3104], eng="pool")
        P.dma(self.g_w2[0:16, :], self.d["gla_w_gate2"][j, 0], eng="pool")
        P.dma(self.g_w2[16:17, :], self.d["gla_b_gate"][j, 0:1, :], eng="pool")
        P.dma(self.g_w2[32:48, :], self.d["gla_w_gate2"][j, 1], eng="pool")
        P.dma(self.g_w2[48:49, :], self.d["gla_b_gate"][j, 1:2, :], eng="pool")
        P.memset(self.g_lrT[:], 1.0, eng="pool")
        for t in range(NT):
            tsl = slice(t * 512, (t + 1) * 512)
            ps = PS[t % 2]
            for c in range(8):
                P.matmul(ps[0:48, :], self.g_wlr[:, c, 0:48], self.hT[:, c, tsl], start=(c == 0), stop=(c == 7))
            P.copy(self.g_lrT[0:16, tsl], ps[0:16, :], eng="dve")
            P.copy(self.g_lrT[32:48, tsl], ps[32:48, :], eng="dve")
        if getattr(self, "dbg_stop", 99) <= 1:
            return
        for h in range(4):
            self.gla_head(j, h, self.task_slot[tb + h])
            self.prefetch()

    def gla_head(self, j, h, slot):
        P = self.P
        PS = self.PS
        win = self.gw_in[slot]
        wout = self.gw_out[slot]
        vtok, oT, qtl, kdc, qk = self.g_vtok, self.g_oT, self.g_qtl, self.g_kdc, self.g_qk
        T = self.tri
        for b in range(16):
            pv = PS[6 + (b // 2) % 2]
            half = b % 2
            for c in range(8):
                P.matmul(pv[:, half * 256:(half + 1) * 256], self.hT[:, c, b * 128:(b + 1) * 128], win[:, c, 256:512],
                         start=(c == 0), stop=(c == 7))
            if half == 1:
                src = pv[:].re("p (a b) -> p a b", a=2)
                P.copy(vtok[:, b - 1:b + 1, :], src, eng=("act" if (b // 2) % 2 == 0 else "dve"))
        bZ, bB, bT, bS, bO = PS[0], PS[1], PS[2], PS[3], PS[4]
        for t in range(NT):
            tsl = slice(t * 512, (t + 1) * 512)
            pq, pk = PS[5], PS[6]
            bZ2 = PS[7]
            for c in range(8):
                P.matmul(pq[:], win[:, c, 0:128], self.hT[:, c, tsl], start=(c == 0), stop=(c == 7))
            for c in range(8):
                P.matmul(pk[:], win[:, c, 128:256], self.hT[:, c, tsl], start=(c == 0), stop=(c == 7))
            P.act(qk[:, 0, :], pq[:], AF.Copy, scale=128.0 ** -0.5)
            P.copy(qk[:, 1, :], pk[:], eng="dve")
            for cl in range(4):
                c = t * 4 + cl
                csl = slice(c * 128, (c + 1) * 128)
                lsl = slice(cl * 128, (cl + 1) * 128)
                la, eq, ek, ekd = self.g_la[c % 2], self.g_eq[c % 2], self.g_ek[c % 2], self.g_ekd[c % 2]
                ktl, sm = self.g_kt[c % 2], self.g_sm[c % 2]
                P.matmul(bZ[:, 0:128], self.g_lrT[0:17, csl], self.g_w2[0:17, h * 128:(h + 1) * 128])
                P.matmul(bZ2[:, 0:128], self.g_lrT[32:49, csl], self.g_w2[32:49, h * 128:(h + 1) * 128])
                P.transpose(bT[:, 0:128], qk[:, 1, lsl], self.ident[:])
                laf = la[:].re("p a b -> p (a b)")
                P.act(la[:, 0, :], bZ[:, 0:128], AF.Exp, scale=-1.0)
                P.act(la[:, 1, :], bZ2[:, 0:128], AF.Exp, scale=-1.0)
                P.act(laf, laf, AF.Ln, bias=self.one_t[:])
                def mm_r(out, lhsT, rhs):
                    P.add("pe", lambda e: e.matmul(out.ap, lhsT.ap.bitcast(mybir.dt.float32r),
                                                   rhs.ap.bitcast(mybir.dt.float32r), start=True, stop=True),
                          reads=[lhsT, rhs], writes=[out])
                mmc = mm_r if getattr(self, "use_f32r", False) else P.matmul
                mmc(bB[:, 0:128], la[:, 0, :], T[0][:])
                mmc(bB[:, 128:256], la[:, 1, :], T[1][:])
                mmc(bB[:, 256:384], T[2][:], la[:, 0, :])
                mmc(bB[:, 384:512], T[3][:], la[:, 1, :])
                P.act(eq[:].re("p a b -> p (a b)"), bB[:, 0:256], AF.Exp, scale=-1.0 / 16)
                P.act(ek[:].re("p a b -> p (a b)"), bB[:, 0:256], AF.Exp, scale=1.0 / 16)
                P.act(ekd[:].re("p a b -> p (a b)"), bB[:, 256:512], AF.Exp, scale=-1.0 / 16)
                P.copy(self.g_eend[:, c:c + 1], eq[:, 0, 127:128], eng="dve")
                P.copy(self.g_eend[:, 16 + c:17 + c], eq[:, 1, 0:1], eng="dve")
                for dr in range(2):
                    P.tt(qtl[:, dr, csl], qk[:, 0, lsl], eq[:, dr, :], ALU.mult)
                    P.tt(ktl[:, dr, :], qk[:, 1, lsl], ek[:, dr, :], ALU.mult)
                for dr in range(2):
                    P.tt(kdc[:, c, dr, :], bT[:, 0:128], ekd[:, dr, :], ALU.mult)
                for dr in range(2):
                    P.matmul(bS[:, dr * 128:(dr + 1) * 128], ktl[:, dr, :], qtl[:, dr, csl])
                P.tt(sm[:, 0, :], bS[:, 0:128], T[0][:], ALU.mult)
                P.tt(sm[:, 1, :], bS[:, 128:256], T[2][:], ALU.mult)
                for vc in range(2):
                    for dr in range(2):
                        P.matmul(bO[:, vc * 128:(vc + 1) * 128], vtok[:, c, vc * 128:(vc + 1) * 128], sm[:, dr, :],
                                 start=(dr == 0), stop=(dr == 1))
                P.copy(oT[:, :, csl], bO[:, 0:256].re("p (a b) -> p a b", a=2), eng="act")
        for step in range(16):
            for dr in range(2):
                c = step if dr == 0 else 15 - step
                csl = slice(c * 128, (c + 1) * 128)
                bI, bK = PS[dr * 2], PS[dr * 2 + 1]
                S = self.g_S[dr]
                if step > 0:
                    Sb = self.g_Sbf[dr][(step - 1) % 2]
                    for vc in range(2):
                        P.matmul(bI[:, vc * 128:(vc + 1) * 128], Sb[:, vc * 128:(vc + 1) * 128], qtl[:, dr, csl])
                    P.tt(oT[:, :, csl], bI[:, 0:256].re("p (a b) -> p a b", a=2), oT[:, :, csl], ALU.add)
                if step < 15:
                    P.matmul(bK[:, 0:256], kdc[:, c, dr, :], vtok[:, c, :])
                    if step == 0:
                        P.copy(S[:], bK[:, 0:256], eng="dve")
                    else:
                        col = dr * 16 + c
                        P.stt(S[:], S[:], self.g_eend[:, col:col + 1], bK[:, 0:256], ALU.mult, ALU.add)
                    P.copy(self.g_Sbf[dr][step % 2][:], S[:], eng="act")
        if getattr(self, "dbg_stop", 99) <= 3:
            return
        for t in range(NT):
            tsl = slice(t * 512, (t + 1) * 512)
            ss, rs = PS[6], PS[7]
            for vc in range(2):
                sq = self.sq[vc]
                P.act(sq[:], oT[:, vc, tsl], AF.Square)
                P.matmul(ss[:], self.ones_bf[:], sq[:], start=(vc == 0), stop=(vc == 1))
            lt = self.lnt[t % 2]
            P.act(lt[:], ss[:], AF.Ln, bias=self.eps_t[:], scale=1.0 / 256)
            P.act(rs[:], lt[:], AF.Exp, scale=-0.5)
            for vc in range(2):
                pg = PS[vc]
                for c in range(8):
                    P.matmul(pg[:], win[:, c, 512 + vc * 128:512 + (vc + 1) * 128], self.hT[:, c, tsl],
                             start=(c == 0), stop=(c == 7))
                P.act(self.g_sg[:], pg[:], AF.Silu)
                P.stt(oT[:, vc, tsl], oT[:, vc, tsl], self.gno[:, j * 2 + vc:j * 2 + vc + 1], rs[:], ALU.mult, ALU.mult)
                P.tt(self.g_onb[:, vc, :], oT[:, vc, tsl], self.g_sg[:], ALU.mult)
            for dc in range(8):
                pb_ = PS[2 + dc % 4]
                for vc in range(2):
                    P.matmul(pb_[:], wout[:, vc, dc * 128:(dc + 1) * 128], self.g_onb[:, vc, :], start=(vc == 0), stop=(vc == 1))
                P.tt(self.xT[:, dc, tsl], pb_[:], self.xT[:, dc, tsl], ALU.add)


    def diff_load(self, j, h, slot):
        P = self.P
        win = self.dw_in[slot]
        W = self.d["diff_w_in"][j].rearrange("(c p) n -> p c n", p=128)
        for q in range(3):
            P.dma(win[:, :, q * 128:(q + 1) * 128], W[:, :, q * 1024 + h * 128:q * 1024 + (h + 1) * 128], eng="pool")

    def diff_load_out(self, j, half, slot):
        P = self.P
        wo = self.d["diff_w_out"][j][half * 512:(half + 1) * 512, :].rearrange("(g p) d -> p g d", p=128)
        P.dma(self.dw_out[slot][:], wo, eng="pool")

    def diff_tasks(self, i):
        j = i // 2
        t = [((lambda slot, h=h: self.diff_load(j, h, slot)), False) for h in range(8)]
        t.append(((lambda slot: self.diff_load_out(j, 0, slot)), False))
        t.append(((lambda slot: self.diff_load_out(j, 1, slot)), False))
        return t

    def diff(self, i, tb):
        import math
        P = self.P
        PS = self.PS
        j = i // 2
        lam_init = 0.8 - 0.6 * math.exp(-0.3 * i)
        dstop = getattr(self, "dbg_dstop", 99)
        if dstop <= 0:
            return
        self.rmsnorm_all((i * 3 + 1) * 8)
        sm = self.d_small
        gq = self.d["diff_qk_norm_g"][j, 0].rearrange("(p o) -> p o", o=1)
        gk = self.d["diff_qk_norm_g"][j, 1].rearrange("(p o) -> p o", o=1)
        P.dma(sm[0:64, 0:1], gq, eng="sp")
        P.dma(sm[64:128, 0:1], gq, eng="sp")
        P.dma(sm[0:64, 1:2], gk, eng="sp")
        P.dma(sm[64:128, 1:2], gk, eng="sp")
        P.dma(sm[:, 2:3], self.d["diff_subln_g"][j].rearrange("(p o) -> p o", o=1), eng="sp")
        P.dma(self.d_lv[:], self.d["diff_lambda"][j].partition_broadcast(128), eng="sp")
        P.ts(sm[:, 3:4], sm[:, 0:1], 64.0 ** -0.5, None, ALU.mult)
        P.ts(sm[:, 4:5], sm[:, 2:3], 1.0 - lam_init, None, ALU.mult)
        lp = self.d_lp
        P.tt(lp[:, 0:64], self.d_lv[:, 0:64], self.d_lv[:, 64:128], ALU.mult)
        P.tt(lp[:, 64:128], self.d_lv[:, 128:192], self.d_lv[:, 192:256], ALU.mult)
        P.add("dve", lambda e: e.reduce_sum(sm[:, 5:6].ap, lp[:, 0:64].ap, mybir.AxisListType.X),
              reads=[lp[:, 0:64]], writes=[sm[:, 5:6]])
        P.add("dve", lambda e: e.reduce_sum(sm[:, 6:7].ap, lp[:, 64:128].ap, mybir.AxisListType.X),
              reads=[lp[:, 64:128]], writes=[sm[:, 6:7]])
        P.act(sm[:, 7:9], sm[:, 5:7], AF.Exp)
        P.tt(sm[:, 9:10], sm[:, 8:9], sm[:, 7:8], ALU.subtract)
        P.ts(sm[:, 10:11], sm[:, 9:10], -lam_init, None, ALU.add)
        gqs, gks, gsub, nlam = sm[:, 3:4], sm[:, 1:2], sm[:, 4:5], sm[:, 10:11]
        if dstop <= 1:
            return

        P.memset(self.d_qz[0][64:128, :], 0.0, eng="pool")
        P.memset(self.d_qz[1][0:64, :], 0.0, eng="pool")
        for h in range(8):
            win = self.dw_in[self.task_slot[tb + h]]
            for which, dst, gcol in ((0, None, 3), (1, self.d_kn, 1)):
                for t in range(NT):
                    tsl = slice(t * 512, (t + 1) * 512)
                    pq = PS[(2 * t) % 4 + 4 * which]
                    ss = PS[(2 * t) % 4 + 1 + 4 * which]
                    for c in range(8):
                        P.matmul(pq[:], win[:, c, which * 128:(which + 1) * 128], self.hT[:, c, tsl],
                                 start=(c == 0), stop=(c == 7))
                    sq = self.sq[t % 2]
                    P.act(sq[:], pq[:], AF.Square)
                    P.matmul(ss[:], self.blk2[:], sq[:])
                    lt = self.lnt[t % 2]
                    P.act(lt[:], ss[:], AF.Ln, bias=self.eps_t[:], scale=1.0 / 64)
                    P.act(self.d_rs[:], lt[:], AF.Exp, scale=-0.5)
                    if which == 0:
                        P.stt(self.d_qz[0][0:64, tsl], pq[0:64, :], sm[0:64, gcol:gcol + 1], self.d_rs[0:64, :], ALU.mult, ALU.mult)
                        P.stt(self.d_qz[1][64:128, tsl], pq[64:128, :], sm[64:128, gcol:gcol + 1], self.d_rs[64:128, :], ALU.mult, ALU.mult)
                    else:
                        P.stt(dst[:, tsl], pq[:], sm[:, gcol:gcol + 1], self.d_rs[:], ALU.mult, ALU.mult)
            if dstop <= 2:
                return
            for b in range(16):
                pv = PS[(b // 4) % 2]
                q4 = b % 4
                for c in range(8):
                    P.matmul(pv[:, q4 * 128:(q4 + 1) * 128], self.hT[:, c, b * 128:(b + 1) * 128], win[:, c, 256:384],
                             start=(c == 0), stop=(c == 7))
                if q4 == 3:
                    src = pv[:].re("p (a b) -> p a b", a=4)
                    P.copy(self.d_vtok[:, b - 3:b + 1, :], src, eng=("act" if (b // 4) % 2 == 0 else "dve"))
            if dstop <= 3:
                return
            src = bass.AP(self.ebscr, h * 128 * 1279 + 127, [[1278, 128], [1, 1152]])
            op = P.dma(self.d_eb[:], src, eng="sp")
            for w in self.eb_writes:
                if w not in op.deps:
                    op.deps.append(w)
                    w.needs_sig = True
            if dstop <= 4:
                P.copy(self.xT[:, 0, :], self.d_qz[0][:], eng="dve")
                P.copy(self.xT[:, 1, :], self.d_kn[:], eng="dve")
                P.copy(self.xT[:, 2, 0:1152], self.d_eb[:], eng="dve")
                P.copy(self.xT[:, 3, 0:16], sm[:], eng="dve")
                P.copy(self.xT[:, 3, 16:32], self.cfar[:], eng="dve")
                P.copy(self.xT[:, 4, :], self.d_vtok[:].re("p a b -> p (a b)"), eng="dve")
                return
            for qt in range(NT):
                self.diff_attn(h, qt, nlam, gsub)
                if dstop <= 5:
                    P.copy(self.xT[:, 0, 0:512], self.d_on[:, 0, 0:512], eng="dve")
                    P.copy(self.xT[:, 1, 0:512], self.d_t[2][:], eng="dve")
                    P.copy(self.xT[:, 2, 0:512], self.d_rz[0][:], eng="dve")
                    P.copy(self.xT[:, 3, 0:512], self.d_rz[1][:], eng="dve")
                    P.copy(self.xT[:, 4, 0:512], self.d_t[0][:], eng="dve")
                    P.copy(self.xT[:, 5, 0:512], self.d_pt[3][:], eng="dve")
                    P.copy(self.xT[:, 6, 0:512], self.d_pt[0][:], eng="dve")
                    return
            self.prefetch()
        wouts = [self.dw_out[self.task_slot[tb + 8]], self.dw_out[self.task_slot[tb + 9]]]
        for t in range(NT):
            tsl = slice(t * 512, (t + 1) * 512)
            for dc in range(8):
                pb = PS[dc % 4 + 4 * (t % 2)]
                for hh in range(8):
                    P.matmul(pb[:], wouts[hh // 4][:, hh % 4, dc * 128:(dc + 1) * 128], self.d_on[:, hh, tsl],
                             start=(hh == 0), stop=(hh == 7))
                P.tt(self.xT[:, dc, tsl], pb[:], self.xT[:, dc, tsl], ALU.add)
        self.prefetch()
        self.prefetch()

    def diff_attn(self, h, qt, nlam, gsub):
        P = self.P
        PS = self.PS
        qsl = slice(qt * 512, (qt + 1) * 512)
        kn, vt = self.d_kn, self.d_vtok
        O = [PS[4], PS[5]]
        Z = [PS[6], PS[7]]

        def S(kc):
            ksl = slice(kc * 128, (kc + 1) * 128)
            for comp in range(2):
                P.matmul(PS[(kc % 2) * 2 + comp][:], kn[:, ksl], self.d_qz[comp][:, qsl])

        S(0)
        for kc in range(16):
            if kc + 1 < 16:
                S(kc + 1)
            delta = kc - 4 * qt
            for comp in range(2):
                sb = PS[(kc % 2) * 2 + comp]
                pt = self.d_pt[(kc % 2) * 2 + comp]
                if -1 <= delta <= 4:
                    e = self.d_e[(2 * kc + comp) % 3]
                    P.act(e[:], sb[:], AF.Exp)
                    off = 512 - 128 * delta
                    P.tt(pt[:], e[:], self.d_eb[:, off:off + 512], ALU.mult)
                else:
                    sgn = 0 if delta < 0 else 1
                    P.act(pt[:], sb[:], AF.Exp, bias=self.cfar[:, 2 * h + sgn:2 * h + sgn + 1])
            for comp in range(2):
                pt = self.d_pt[(kc % 2) * 2 + comp]
                P.matmul(O[comp][:], vt[:, kc, :], pt[:], start=(kc == 0), stop=(kc == 15))
                P.matmul(Z[comp][:], self.ones_bf[:], pt[:], start=(kc == 0), stop=(kc == 15))
        t0, t1, t2 = self.d_t
        for comp in range(2):
            lt = self.lnt[comp]
            P.act(lt[:], Z[comp][:], AF.Ln)
            P.act(self.d_rz[comp][:], lt[:], AF.Exp, scale=-1.0)
        P.tt(t0[:], O[0][:], self.d_rz[0][:], ALU.mult)
        P.stt(t1[:], O[1][:], nlam, self.d_rz[1][:], ALU.mult, ALU.mult)
        P.tt(t2[:], t0[:], t1[:], ALU.add)
        sq = self.sq[0]
        P.act(sq[:], t2[:], AF.Square)
        ss = PS[0]
        P.matmul(ss[:], self.ones_bf[:], sq[:])
        lt = self.lnt[0]
        P.act(lt[:], ss[:], AF.Ln, bias=self.eps_t[:], scale=1.0 / 128)
        P.act(self.d_rs[:], lt[:], AF.Exp, scale=-0.5)
        P.stt(self.d_on[:, h, qsl], t2[:], gsub, self.d_rs[:], ALU.mult, ALU.mult)

    def build(self):
        P = self.P
        P.memset(self.eps_t[:], EPS, eng="pool")
        P.memset(self.one_t[:], 1.0, eng="pool")
        self.eb_writes = []
        self.setup()
        if any(st[0] == "diff" for st in self.plan):
            self.setup_bias()
        bases = []
        for s in range(self.n_seq):
            for step in self.plan:
                bases.append(len(self.load_q))
                if step[0] == "ffn":
                    self.load_q += self.ffn_tasks(step[1], step[2])
                elif step[0] == "gla":
                    self.load_q += self.gla_tasks(step[1])
                elif step[0] == "diff":
                    self.load_q += self.diff_tasks(step[1])
                elif step[0] == "dbg_gate":
                    self.load_q += self.ffn_tasks(0, 0)[:1]
        self.pump()
        k = 0
        for s in range(self.n_seq):
            self.load_x(s)
            for step in self.plan:
                tb = bases[k]
                k += 1
                if step[0] == "ffn":
                    self.ffn(step[1], step[2], tb)
                elif step[0] == "gla":
                    self.in_mixer = True
                    self.gla(step[1], tb)
                    self.in_mixer = False
                    self.pump()
                elif step[0] == "diff":
                    self.in_mixer = True
                    self.diff(step[1], tb)
                    self.in_mixer = False
                    self.pump()
                elif step[0] == "dbg_gate":
                    self.rmsnorm_all(0)
                    wv = self.wv[self.task_slot[tb]]
                    tsl = slice(0, 512)
                    pg, pu = self.PS[0], self.PS[1]
                    for c in range(8):
                        P.matmul(pg[:], wv["gate"][:, c, 0:128], self.hT[:, c, tsl], start=(c == 0), stop=(c == 7))
                    for c in range(8):
                        P.matmul(pu[:], wv["up"][:, c, 0:128], self.hT[:, c, tsl], start=(c == 0), stop=(c == 7))
                    P.copy(self.xT[:, 0, tsl], pg[:], eng="dve")
                    P.copy(self.xT[:, 2, tsl], pu[:], eng="dve")
                    P.act(self.sg[0][:], pg[:], AF.Silu)
                    P.copy(self.xT[:, 1, tsl], self.sg[0][:], eng="dve")
                    P.tt(self.aT[0][:, 0, :], pu[:], self.sg[0][:], ALU.mult)
                    P.copy(self.xT[:, 3, tsl], self.aT[0][:, 0, :], eng="dve")
                    P.copy(self.xT[:, 4, tsl], wv["gate"][:, 0, 0:512], eng="dve")
                    P.copy(self.xT[:, 5, tsl], wv["down"][:, 0, 0:512], eng="dve")
                elif step[0] == "dbg_norm":
                    self.rmsnorm_all(step[1])
                    for c in range(8):
                        P.copy(self.xT[:, c, :], self.hT[:, c, :], eng="dve")
            self.store_x(s)
        return P.emit()


FULL_PLAN = []
for _i in range(DEPTH):
    FULL_PLAN += [("ffn", _i, 0), ("gla" if _i % 2 == 0 else "diff", _i), ("ffn", _i, 1)]

_CONSTS = None


def _t5_onehot():
    import math
    import jax
    import jax.numpy as jnp
    cpu = jax.devices("cpu")[0]
    with jax.default_device(cpu):
        u = jnp.arange(1279, dtype=jnp.int32)
        rel = 639 - u
        nb = 16
        max_exact = 8
        ret = (rel > 0).astype(jnp.int32) * nb
        n = jnp.abs(rel)
        nf = jnp.maximum(n, 1).astype(jnp.float32)
        large = max_exact + (jnp.log(nf / max_exact) / math.log(128 / max_exact) * (nb - max_exact)).astype(jnp.int32)
        large = jnp.minimum(large, nb - 1)
        bucket = np.asarray(ret + jnp.where(n < max_exact, n, large))
    oh = np.zeros((32, 1281), np.float32)
    oh[bucket, np.arange(1279)] = 1.0
    oh[15, 1279] = 1.0
    oh[31, 1280] = 1.0
    return oh


def _consts():
    global _CONSTS
    if _CONSTS is None:
        c = {}
        c["c_ident"] = np.eye(128, dtype=np.float32)
        r = np.arange(128)[:, None]
        q = np.arange(128)[None, :]
        c["c_tri"] = np.stack([(r <= q), (r >= q), (r > q), (r < q)]).astype(np.float32)
        c["c_oh"] = _t5_onehot()
        _CONSTS = c
    return _CONSTS


def run_plan(inputs, plan, n_seq=2, n_cores=8, trace=False, decl_depth=DEPTH, **dbg):
    nc = bass.Bass("TRN2", target_bir_lowering=False)
    k = K(nc, n_seq, plan, decl_depth, **dbg)
    stats = k.build()
    print("program stats:", stats, flush=True)
    x = np.ascontiguousarray(np.asarray(inputs["x"], dtype=np.float32))
    shared = {}
    for name in k.d:
        if name == "x" or name.startswith("c_"):
            continue
        a = np.ascontiguousarray(np.asarray(inputs[name], dtype=np.float32))
        shp = tuple(k.d[name].shape)
        if a.size != int(np.prod(shp)):
            a = a[:shp[0]]
        shared[name] = np.ascontiguousarray(a).reshape(shp)
    shared.update({n: v for n, v in _consts().items() if n in k.d})
    in_maps = []
    for c in range(n_cores):
        m = dict(shared)
        m["x"] = x[c * n_seq:(c + 1) * n_seq]
        in_maps.append(m)
    res = run_bass_kernel_spmd(nc, in_maps, core_ids=list(range(n_cores)), trace=trace)
    out = np.concatenate([r["y"] for r in res.results], axis=0)
    return out, res


def kernel(**inputs):
    out, _ = run_plan(inputs, FULL_PLAN, n_seq=2, n_cores=8)
    return out.astype(np.float32)
```

```python
import numpy as np
import concourse.bass as bass
import concourse.mybir as mybir

F32 = mybir.dt.float32
BF16 = mybir.dt.bfloat16
ALU = mybir.AluOpType
AF = mybir.ActivationFunctionType
DTSIZE = {F32: 4, BF16: 2}

CELL = 128
SEM_LIMIT = 12000
N_DMA_SEMS = 24
SAME_ENGINE_SYNC = True
SB_BASE = 16512
SB_END = 229344


class Acc:
    __slots__ = ("ap", "rng")

    def __init__(self, ap, rng):
        self.ap = ap
        self.rng = rng

    def re(self, pattern, **kw):
        return Acc(self.ap.rearrange(pattern, **kw), self.rng)


class Buf:
    def __init__(self, prog, name, shape, dtype, space, base, handle):
        self.prog = prog
        self.name = name
        self.shape = list(shape)
        self.dtype = dtype
        self.esz = DTSIZE[dtype]
        self.space = space
        self.base = base
        self.t = handle
        fs = self.shape[1:]
        st = [1] * len(fs)
        for i in range(len(fs) - 2, -1, -1):
            st[i] = st[i + 1] * fs[i + 1]
        self.strides = st

    def __getitem__(self, idx):
        if not isinstance(idx, tuple):
            idx = (idx,)
        idx = list(idx) + [slice(None)] * (len(self.shape) - len(idx))
        ap = self.t[tuple(idx)]
        fidx = idx[1:]
        fs = self.shape[1:]
        lohi = []
        for k, ix in enumerate(fidx):
            if isinstance(ix, slice):
                lo = 0 if ix.start is None else ix.start
                hi = fs[k] if ix.stop is None else ix.stop
                assert ix.step in (None, 1)
            else:
                lo, hi = ix, ix + 1
            assert 0 <= lo < hi <= fs[k], (self.name, idx, self.shape)
            lohi.append((lo, hi))
        n = len(fs)
        k = n - 1
        run = 1
        while k >= 0 and lohi[k] == (0, fs[k]):
            run *= fs[k]
            k -= 1
        ranges = []
        if k < 0:
            ranges.append((0, run))
        else:
            outer = lohi[:k]
            lo_k, hi_k = lohi[k]
            seg = (hi_k - lo_k) * self.strides[k]

            def rec(d, off):
                if d == k:
                    s = off + lo_k * self.strides[k]
                    ranges.append((s, s + seg))
                    return
                for i in range(outer[d][0], outer[d][1]):
                    rec(d + 1, off + i * self.strides[d])
            rec(0, 0)
        if self.space == "ps":
            b = self.base // 2048
            return Acc(ap, [("ps", b, b + 1)])
        rng = []
        for (s, e) in ranges:
            b0 = self.base + s * self.esz
            b1 = self.base + e * self.esz
            rng.append((self.space, b0 // CELL, (b1 - 1) // CELL + 1))
        return Acc(ap, rng)


class Op:
    __slots__ = ("eng", "fn", "deps", "is_dma", "sig", "needs_sig", "idx", "tag")

    def __init__(self, eng, fn, is_dma, tag):
        self.eng = eng
        self.fn = fn
        self.deps = []
        self.is_dma = is_dma
        self.sig = None
        self.needs_sig = False
        self.tag = tag


class Prog:
    ENGS = ("pe", "act", "dve", "pool", "sp")

    def __init__(self, nc):
        self.nc = nc
        self.ops = {e: [] for e in self.ENGS}
        self.nops = 0
        self.sb_off = SB_BASE
        self.ps_bank = 0
        self.lastw = {"sb": {}, "ps": {}}
        self.readers = {"sb": {}, "ps": {}}
        self.sems = []
        self.final_waits = []
        self.n_bufs = 0

    def sbuf(self, name, shape, dtype, at=None, align=CELL):
        esz = DTSIZE[dtype]
        nbytes = int(np.prod(shape[1:])) * esz
        if at is None:
            off = (self.sb_off + align - 1) // align * align
            self.sb_off = off + nbytes
            assert self.sb_off <= SB_END, (name, self.sb_off)
        else:
            off = at
        self.n_bufs += 1
        h = self.nc.alloc_sbuf_tensor_at(f"{name}_{self.n_bufs}", list(shape), dtype, offset=off)
        return Buf(self, name, shape, dtype, "sb", off, h)

    def psum_bank(self, name):
        b = self.ps_bank
        self.ps_bank += 1
        assert b < 8
        h = self.nc.alloc_psum_tensor(f"{name}_{b}", [128, 512], F32)
        return Buf(self, name, [128, 512], F32, "ps", b * 2048, h)

    def add(self, eng, fn, reads=(), writes=(), is_dma=False, tag=None):
        op = Op(eng, fn, is_dma, tag)
        op.idx = self.nops
        self.nops += 1
        deps = set()
        ps_reads = [a for a in reads if a.rng and a.rng[0][0] == "ps"]
        if ps_reads:
            reads = [a for a in reads if not (a.rng and a.rng[0][0] == "ps")]
            writes = list(writes) + ps_reads
        for a in reads:
            for (sp, c0, c1) in a.rng:
                lw = self.lastw[sp]
                for c in range(c0, c1):
                    w = lw.get(c)
                    if w is not None:
                        deps.add(w)
        for a in writes:
            for (sp, c0, c1) in a.rng:
                lw = self.lastw[sp]
                rd = self.readers[sp]
                for c in range(c0, c1):
                    w = lw.get(c)
                    if w is not None:
                        deps.add(w)
                    r = rd.get(c)
                    if r:
                        deps.update(r)
        for a in reads:
            for (sp, c0, c1) in a.rng:
                rd = self.readers[sp]
                for c in range(c0, c1):
                    l = rd.get(c)
                    if l is None:
                        rd[c] = [op]
                    elif l[-1] is not op:
                        l.append(op)
        for a in writes:
            for (sp, c0, c1) in a.rng:
                lw = self.lastw[sp]
                rd = self.readers[sp]
                for c in range(c0, c1):
                    lw[c] = op
                    if c in rd:
                        rd[c] = None
        deps.discard(op)
        for d in deps:
            if d.eng == eng and not d.is_dma:
                if eng == "pe" or not SAME_ENGINE_SYNC:
                    continue
            op.deps.append(d)
            d.needs_sig = True
        self.ops[eng].append(op)
        return op

    def matmul(self, out, lhsT, rhs, start=True, stop=True, **kw):
        return self.add("pe", lambda e: e.matmul(out.ap, lhsT.ap, rhs.ap, start=start, stop=stop, **kw),
                        reads=[lhsT, rhs], writes=[out])

    def transpose(self, out, in_, ident):
        return self.add("pe", lambda e: e.transpose(out.ap, in_.ap, ident.ap),
                        reads=[in_, ident], writes=[out])

    def act(self, out, in_, func, bias=None, scale=1.0, accum_out=None, eng="act"):
        reads = [in_]
        kw = {}
        if bias is not None:
            if isinstance(bias, Acc):
                reads.append(bias)
                kw["bias"] = bias.ap
            else:
                kw["bias"] = bias
        if isinstance(scale, Acc):
            reads.append(scale)
            kw["scale"] = scale.ap
        else:
            kw["scale"] = scale
        writes = [out]
        if accum_out is not None:
            writes.append(accum_out)
            kw["accum_out"] = accum_out.ap
        return self.add(eng, lambda e: e.activation(out.ap, in_.ap, func, **kw), reads=reads, writes=writes)

    def tt(self, out, in0, in1, op, eng="dve"):
        return self.add(eng, lambda e: e.tensor_tensor(out.ap, in0.ap, in1.ap, op), reads=[in0, in1], writes=[out])

    def ts(self, out, in0, s1, s2, op0, op1=None, eng="dve"):
        reads = [in0]
        a1 = s1.ap if isinstance(s1, Acc) else s1
        a2 = s2.ap if isinstance(s2, Acc) else s2
        if isinstance(s1, Acc):
            reads.append(s1)
        if isinstance(s2, Acc):
            reads.append(s2)
        if op1 is None:
            return self.add(eng, lambda e: e.tensor_scalar(out.ap, in0.ap, a1, None, op0), reads=reads, writes=[out])
        return self.add(eng, lambda e: e.tensor_scalar(out.ap, in0.ap, a1, a2, op0, op1), reads=reads, writes=[out])

    def stt(self, out, in0, scalar, in1, op0, op1, eng="dve"):
        reads = [in0, in1]
        sc = scalar.ap if isinstance(scalar, Acc) else scalar
        if isinstance(scalar, Acc):
            reads.append(scalar)
        return self.add(eng, lambda e: e.scalar_tensor_tensor(out.ap, in0.ap, sc, in1.ap, op0, op1),
                        reads=reads, writes=[out])

    def copy(self, out, in_, eng="dve"):
        if eng == "act":
            return self.add("act", lambda e: e.copy(out.ap, in_.ap), reads=[in_], writes=[out])
        return self.add(eng, lambda e: e.tensor_copy(out.ap, in_.ap), reads=[in_], writes=[out])

    def recip(self, out, in_):
        return self.add("dve", lambda e: e.reciprocal(out.ap, in_.ap), reads=[in_], writes=[out])

    def memset(self, out, val, eng="pool"):
        return self.add(eng, lambda e: e.memset(out.ap, val), reads=[], writes=[out])

    def dma(self, out, in_, eng="sp", final=False, **kw):
        reads = [in_] if isinstance(in_, Acc) else []
        writes = [out] if isinstance(out, Acc) else []
        oap = out.ap if isinstance(out, Acc) else out
        iap = in_.ap if isinstance(in_, Acc) else in_
        op = self.add(eng, lambda e: e.dma_start(out=oap, in_=iap, **kw), reads=reads, writes=writes, is_dma=True)
        if final:
            op.needs_sig = True
            self.final_waits.append(op)
        return op

    def _new_sem(self, name):
        cm = self.nc.semaphore(f"{name}_{len(self.sems)}")
        h = cm.__enter__()
        self.sems.append(cm)
        return h

    def emit(self):
        nc = self.nc
        dma_sems = [[self._new_sem("dma"), 0, None] for _ in range(N_DMA_SEMS)]
        dma_rr = 0
        all_dma = sorted([op for e in self.ENGS for op in self.ops[e] if op.is_dma], key=lambda o: o.idx)
        for op in all_dma:
            slot = dma_sems[dma_rr % N_DMA_SEMS]
            dma_rr += 1
            if slot[1] + 16 > SEM_LIMIT:
                slot[0] = self._new_sem("dma")
                slot[1] = 0
                slot[2] = None
            if slot[2] is not None:
                prev = slot[2]
                if prev not in op.deps:
                    op.deps.append(prev)
            slot[1] += 16
            op.sig = (slot[0], slot[1])
            slot[2] = op
        for e in self.ENGS:
            sem = None
            cnt = 0
            for op in self.ops[e]:
                if op.is_dma or not op.needs_sig:
                    continue
                if sem is None or cnt + 1 > SEM_LIMIT:
                    sem = self._new_sem(e)
                    cnt = 0
                cnt += 1
                op.sig = (sem, cnt)
        nwaits = {e: 0 for e in self.ENGS}

        def emit_engine(ename, eng):
            seen = {}
            for op in self.ops[ename]:
                need = {}
                for d in op.deps:
                    s, v = d.sig
                    if seen.get(s, 0) >= v:
                        continue
                    if need.get(s, (None, 0))[1] < v:
                        need[s] = (s, v)
                for (s, v) in need.values():
                    eng.wait_ge(s, v)
                    seen[s] = v
                    nwaits[ename] += 1
                ins = op.fn(eng)
                if op.sig is not None:
                    ins.then_inc(op.sig[0], 16 if op.is_dma else 1)
            if ename == "sp":
                for op in self.final_waits:
                    s, v = op.sig
                    if seen.get(s, 0) < v:
                        eng.wait_ge(s, v)
                        seen[s] = v

        with nc.Block() as block:
            @block.tensor
            def _(e):
                emit_engine("pe", e)

            @block.scalar
            def _(e):
                emit_engine("act", e)

            @block.vector
            def _(e):
                emit_engine("dve", e)

            @block.gpsimd
            def _(e):
                emit_engine("pool", e)

            @block.sync
            def _(e):
                emit_engine("sp", e)
        self.stats = {e: (len(self.ops[e]), nwaits[e]) for e in self.ENGS}
        self.stats["sems"] = len(self.sems)
        return self.stats

from concourse.bass_utils import run_bass_kernel_spmd

D_MODEL = 1024
SEQ = 2048
DEPTH = 4
D_FF = 2816
NF = D_FF // 128
EPS = 1e-6
FFN_GROUPS = [(0, 4), (4, 4), (8, 4), (12, 4), (16, 3), (19, 3)]
NT = SEQ // 512


class K:
    def __init__(self, nc, n_seq, plan, decl_depth=DEPTH, **dbg):
        self.nc = nc
        for k_, v_ in dbg.items():
            setattr(self, k_, v_)
        self.n_seq = n_seq
        self.plan = plan
        P = self.P = Prog(nc)
        d = self.d = {}

        def din(name, shape):
            d[name] = nc.dram_tensor(name, list(shape), F32, kind="ExternalInput").ap()
        din("x", [n_seq, SEQ, D_MODEL])
        din("norm_g", [DEPTH * 3 * 8, 128])
        din("ffn_w_gu", [decl_depth, 2, D_MODEL, 2 * D_FF])
        din("ffn_w_down", [decl_depth, 2, D_FF, D_MODEL])
        din("c_ident", [128, 128])
        din("c_tri", [4, 128, 128])
        din("gla_w_in", [max(1, decl_depth // 2), D_MODEL, 3104])
        din("gla_w_gate2", [max(1, decl_depth // 2), 2, 16, 512])
        din("gla_b_gate", [max(1, decl_depth // 2), 2, 512])
        din("gla_o_norm_g", [4, 128])
        din("gla_w_out", [max(1, decl_depth // 2), 1024, 1024])
        din("c_oh", [32, 1281])
        din("diff_w_in", [max(1, decl_depth // 2), D_MODEL, 3072])
        din("diff_qk_norm_g", [max(1, decl_depth // 2), 2, 64])
        din("diff_lambda", [max(1, decl_depth // 2), 256])
        din("diff_subln_g", [max(1, decl_depth // 2), 128])
        din("diff_w_out", [max(1, decl_depth // 2), 1024, 1024])
        din("rel_bias_table", [32, 8])
        self.ebscr = nc.dram_tensor("ebscr", [8 * 128 * 1279], F32)
        self.y = nc.dram_tensor("y", [n_seq, SEQ, D_MODEL], F32, kind="ExternalOutput").ap()

        self.xT = P.sbuf("xT", [128, 8, SEQ], F32)
        self.hT = P.sbuf("hT", [128, 8, SEQ], BF16)
        self.ident = P.sbuf("ident", [128, 128], F32)
        self.ones_bf = P.sbuf("ones_bf", [128, 128], BF16)
        self.gn = P.sbuf("gn", [128, DEPTH * 3 * 8], F32)
        self.tri = [P.sbuf(f"tri{i}", [128, 128], F32) for i in range(4)]
        self.eps_t = P.sbuf("eps_t", [128, 1], F32)
        self.blk2 = P.sbuf("blk2", [128, 128], BF16)
        self.cfar = P.sbuf("cfar", [128, 16], F32)
        self.one_t = P.sbuf("one_t", [128, 1], F32)
        self.gno = P.sbuf("gno", [128, 4], F32)
        self.wslot_off = []
        for i in range(2):
            b = P.sbuf(f"wslot{i}", [128, 12288], BF16)
            self.wslot_off.append(b.base)
        self.wv = []
        for i in range(2):
            o = self.wslot_off[i]
            self.wv.append(dict(
                gate=P.sbuf(f"wg{i}", [128, 8, 512], BF16, at=o),
                up=P.sbuf(f"wu{i}", [128, 8, 512], BF16, at=o + 8192),
                down=P.sbuf(f"wd{i}", [128, 4, 1024], BF16, at=o + 16384),
            ))
        self.gw_in = [P.sbuf(f"gwin{i}", [128, 8, 768], BF16, at=self.wslot_off[i]) for i in range(2)]
        self.gw_out = [P.sbuf(f"gwout{i}", [128, 2, 1024], BF16, at=self.wslot_off[i] + 12288) for i in range(2)]
        self.sq = [P.sbuf(f"sq{i}", [128, 512], BF16) for i in range(2)]
        self.lnt = [P.sbuf(f"lnt{i}", [128, 512], F32) for i in range(2)]
        arena = P.sb_off
        print("arena base", arena, "bytes avail", SB_END - arena, flush=True)
        self.stage = [P.sbuf(f"stage{i}", [128, 1024], F32) for i in range(2)]
        P.sb_off = arena
        self.aT = [P.sbuf(f"aT{i}", [128, 4, 512], BF16) for i in range(2)]
        self.sg = [P.sbuf(f"sg{i}", [128, 512], F32) for i in range(2)]
        P.sb_off = arena
        self.g_vtok = P.sbuf("g_vtok", [128, 16, 256], BF16)
        self.g_oT = P.sbuf("g_oT", [128, 2, SEQ], F32)
        self.g_lrT = P.sbuf("g_lrT", [64, SEQ], BF16)
        self.g_onb = P.sbuf("g_onb", [128, 2, 512], BF16)
        self.g_qtl = P.sbuf("g_qtl", [128, 2, SEQ], BF16)
        self.g_kdc = P.sbuf("g_kdc", [128, 16, 2, 128], BF16)
        self.g_qk = P.sbuf("g_qk", [128, 2, 512], F32)
        self.g_eend = P.sbuf("g_eend", [128, 32], F32)
        self.g_sg = P.sbuf("g_sg", [128, 512], F32, at=self.g_qk.base)
        gla_end = P.sb_off
        P.sb_off = self.wslot_off[0] + 16384
        self.g_la = [P.sbuf(f"g_la{i}", [128, 2, 128], F32) for i in range(2)]
        self.g_eq = [P.sbuf(f"g_eq{i}", [128, 2, 128], F32) for i in range(2)]
        self.g_ek = [P.sbuf(f"g_ek{i}", [128, 2, 128], F32) for i in range(2)]
        self.g_w2 = P.sbuf("g_w2", [64, 512], BF16)
        self.g_wlr = P.sbuf("g_wlr", [128, 8, 64], BF16)
        assert P.sb_off <= self.wslot_off[0] + 24576, P.sb_off - self.wslot_off[0]
        P.sb_off = self.wslot_off[1] + 16384
        self.g_ekd = [P.sbuf(f"g_ekd{i}", [128, 2, 128], F32) for i in range(2)]
        self.g_kt = [P.sbuf(f"g_kt{i}", [128, 2, 128], BF16) for i in range(2)]
        self.g_sm = [P.sbuf(f"g_sm{i}", [128, 2, 128], BF16) for i in range(2)]
        self.g_S = [P.sbuf(f"g_S{i}", [128, 256], F32) for i in range(2)]
        self.g_Sbf = [[P.sbuf(f"g_Sbf{i}{k}", [128, 256], BF16) for k in range(2)] for i in range(2)]
        assert P.sb_off <= self.wslot_off[1] + 24576, P.sb_off - self.wslot_off[1]
        P.sb_off = gla_end
        print("gla arena end", P.sb_off, flush=True)
        P.sb_off = arena
        self.d_on = P.sbuf("d_on", [128, 8, SEQ], BF16)
        self.d_qz = [P.sbuf(f"d_qz{i}", [128, SEQ], BF16) for i in range(2)]
        self.d_kn = P.sbuf("d_kn", [128, SEQ], BF16)
        self.d_vtok = P.sbuf("d_vtok", [128, 16, 128], BF16)
        print("diff arena end", P.sb_off, flush=True)
        dend = P.sb_off
        P.sb_off = self.wslot_off[0] + 8192
        self.d_e = [P.sbuf(f"d_e{i}", [128, 512], F32) for i in range(3)]
        self.d_pt = [P.sbuf(f"d_pt{i}", [128, 512], BF16) for i in range(4)]
        self.d_t = [P.sbuf(f"d_t{i}", [128, 512], F32) for i in range(3)]
        assert P.sb_off <= self.wslot_off[0] + 24576
        P.sb_off = self.wslot_off[1] + 8192
        self.d_rs = P.sbuf("d_rs", [128, 512], F32)
        self.d_eb = P.sbuf("d_eb", [128, 1152], F32)
        self.d_rz = [P.sbuf(f"d_rz{i}", [128, 512], F32) for i in range(2)]
        self.d_small = P.sbuf("d_small", [128, 16], F32)
        self.d_lv = P.sbuf("d_lv", [128, 256], F32)
        self.d_lp = P.sbuf("d_lp", [128, 128], F32)
        assert P.sb_off <= self.wslot_off[1] + 24576
        P.sb_off = max(dend, gla_end)
        self.dw_in = [P.sbuf(f"dwin{i}", [128, 8, 384], BF16, at=self.wslot_off[i]) for i in range(2)]
        self.dw_out = [P.sbuf(f"dwout{i}", [128, 4, 1024], BF16, at=self.wslot_off[i]) for i in range(2)]
        self.PS = [P.psum_bank("ps") for _ in range(8)]
        self.load_q = []
        self.load_issued = 0
        self.tasks_done = 0
        self.in_mixer = False
        self.norm_done = False
        self.next_gcol = None
        self.task_slot = {}

    def prefetch(self):
        self.tasks_done += 1
        self.pump()

    def pump(self):
        while self.load_issued < min(len(self.load_q), self.tasks_done + 2):
            k = self.load_issued
            fn, full = self.load_q[k]
            if full and self.in_mixer:
                return
            self.load_issued += 1
            self.task_slot[k] = k % 2
            fn(k % 2)

    def setup(self):
        P = self.P
        P.dma(self.ident[:], self.d["c_ident"], eng="sp")
        P.memset(self.ones_bf[:], 1.0, eng="pool")
        st = self.stage[0]
        P.dma(st[0:96, 0:128], self.d["norm_g"], eng="sp")
        ps = self.PS[0]
        P.transpose(ps[:, 0:96], st[0:96, 0:128], self.ident[0:96, 0:96])
        P.copy(self.gn[:], ps[:, 0:96], eng="dve")
        for q in range(4):
            P.dma(self.tri[q][:], self.d["c_tri"][q], eng="sp")
        st1 = self.stage[1]
        P.dma(st1[0:4, 0:128], self.d["gla_o_norm_g"], eng="sp")
        ps1 = self.PS[1]
        P.transpose(ps1[:, 0:4], st1[0:4, 0:128], self.ident[0:4, 0:4])
        P.copy(self.gno[:], ps1[:, 0:4], eng="dve")


    def setup_bias(self):
        P = self.P
        PS = self.PS
        st = self.stage
        P.memset(self.blk2[:], 0.0, eng="pool")
        P.memset(self.blk2[0:64, 0:64], 1.0, eng="pool")
        P.memset(self.blk2[64:128, 64:128], 1.0, eng="pool")
        oh = st[0]
        oh2 = st[1]
        P.dma(oh[0:32, 0:1024], self.d["c_oh"][:, 0:1024], eng="sp")
        P.dma(oh2[0:32, 0:257], self.d["c_oh"][:, 1024:1281], eng="sp")
        P.dma(oh2[0:32, 512:520], self.d["rel_bias_table"], eng="sp")
        P.memset(oh2[0:32, 640:768], 1.0, eng="dve")
        sbs = getattr(self, "dbg_sbs", 99)
        if sbs <= 0:
            return
        for h in range(8):
            tb = oh2[0:32, 768:896]
            P.ts(tb, oh2[0:32, 640:768], oh2[0:32, 512 + h:513 + h], None, ALU.mult)
            b0, b1, b2 = PS[(3 * h) % 8], PS[(3 * h + 1) % 8], PS[(3 * h + 2) % 8]
            if sbs <= 1:
                continue
            P.matmul(b0[:, 0:512], tb, oh[0:32, 0:512])
            P.matmul(b1[:, 0:512], tb, oh[0:32, 512:1024])
            P.matmul(b2[:, 0:257], tb, oh2[0:32, 0:257])
            if sbs <= 2:
                continue
            e = self.g_oT
            P.act(e[:, 0, 0:512], b0[:, 0:512], AF.Exp)
            P.act(e[:, 0, 512:1024], b1[:, 0:512], AF.Exp)
            P.act(e[:, 0, 1024:1279], b2[:, 0:255], AF.Exp)
            P.copy(self.cfar[:, 2 * h:2 * h + 2], b2[:, 255:257], eng="dve")
            if sbs <= 3:
                continue
            dst = bass.AP(self.ebscr, h * 128 * 1279, [[1279, 128], [1, 1279]])
            op = P.dma(dst, e[:, 0, 0:1279], eng="sp")
            self.eb_writes.append(op)

    def load_x(self, s):
        P = self.P
        for b in range(SEQ // 128):
            st = self.stage[b % 2]
            P.dma(st[:], self.d["x"][s, b * 128:(b + 1) * 128, :], eng="sp")
            for half in range(2):
                ps = self.PS[(2 * b + half) % 4]
                for cc in range(4):
                    c = half * 4 + cc
                    P.transpose(ps[:, cc * 128:(cc + 1) * 128], st[:, c * 128:(c + 1) * 128], self.ident[:])
                dst = self.xT[:, half * 4:half * 4 + 4, b * 128:(b + 1) * 128]
                src = ps[:].re("p (a b) -> p a b", a=4)
                if half == 0:
                    P.copy(dst, src, eng="dve")
                else:
                    P.copy(dst, src, eng="act")

    def store_x(self, s):
        P = self.P
        for b in range(SEQ // 128):
            st = self.stage[b % 2]
            for half in range(2):
                ps = self.PS[(2 * b + half) % 4]
                for cc in range(4):
                    c = half * 4 + cc
                    P.transpose(ps[:, cc * 128:(cc + 1) * 128], self.xT[:, c, b * 128:(b + 1) * 128], self.ident[:])
                dst = st[:, half * 512:(half + 1) * 512]
                if half == 0:
                    P.copy(dst, ps[:], eng="dve")
                else:
                    P.copy(dst, ps[:], eng="act")
            P.dma(self.y[s, b * 128:(b + 1) * 128, :], st[:], eng="sp", final=True)

    def rmsnorm_tile(self, t, gcol, inter=False):
        P = self.P
        tsl = slice(t * 512, (t + 1) * 512)
        if inter:
            ss = self.PS[7]
            rs = self.lnt[1]
            lt = self.lnt[0]
        else:
            ss = self.PS[4 + (t % 2) * 2]
            rs = self.PS[5 + (t % 2) * 2]
            lt = self.lnt[t % 2]
        for c in range(8):
            sq = self.sq[c % 2]
            P.act(sq[:], self.xT[:, c, tsl], AF.Square)
            P.matmul(ss[:], self.ones_bf[:], sq[:], start=(c == 0), stop=(c == 7))
        P.act(lt[:], ss[:], AF.Ln, bias=self.eps_t[:], scale=1.0 / D_MODEL)
        P.act(rs[:], lt[:], AF.Exp, scale=-0.5)
        for c in range(8):
            P.stt(self.hT[:, c, tsl], self.xT[:, c, tsl], self.gn[:, gcol + c:gcol + c + 1], rs[:],
                  ALU.mult, ALU.mult)

    def rmsnorm_all(self, gcol):
        if self.norm_done:
            self.norm_done = False
            return
        for t in range(NT):
            self.rmsnorm_tile(t, gcol)

    def ffn_load(self, i, j, gi, slot):
        P = self.P
        f0, G = FFN_GROUPS[gi]
        wv = self.wv[slot]
        wgu = self.d["ffn_w_gu"][i, j].rearrange("(c p) n -> p c n", p=128)
        P.dma(wv["gate"][:, :, 0:G * 128], wgu[:, :, f0 * 128:(f0 + G) * 128], eng="pool")
        P.dma(wv["up"][:, :, 0:G * 128], wgu[:, :, D_FF + f0 * 128:D_FF + (f0 + G) * 128], eng="pool")
        wd = self.d["ffn_w_down"][i, j][f0 * 128:(f0 + G) * 128, :].rearrange("(g p) d -> p g d", p=128)
        P.dma(wv["down"][:, 0:G, :], wd, eng="pool")

    def ffn_tasks(self, i, j):
        return [((lambda slot, gi=gi: self.ffn_load(i, j, gi, slot)), True) for gi in range(len(FFN_GROUPS))]

    def ffn(self, i, j, tbase):
        P = self.P
        gcol = (i * 3 + (0 if j == 0 else 2)) * 8
        self.rmsnorm_all(gcol)
        ngroups = len(FFN_GROUPS)
        slots = self.task_slot
        units = [(gi, t, fi) for gi in range(ngroups) for t in range(NT) for fi in range(FFN_GROUPS[gi][1])]
        ucount = [0]
        dcount = [0]

        def U(n):
            gi, t, fi = units[n]
            wv = self.wv[slots[tbase + gi]]
            tsl = slice(t * 512, (t + 1) * 512)
            pg = self.PS[(n % 2) * 2]
            pu = self.PS[(n % 2) * 2 + 1]
            for c in range(8):
                P.matmul(pg[:], wv["gate"][:, c, fi * 128:(fi + 1) * 128], self.hT[:, c, tsl], start=(c == 0), stop=(c == 7))
            for c in range(8):
                P.matmul(pu[:], wv["up"][:, c, fi * 128:(fi + 1) * 128], self.hT[:, c, tsl], start=(c == 0), stop=(c == 7))

        def E(n):
            gi, t, fi = units[n]
            G = FFN_GROUPS[gi][1]
            pg = self.PS[(n % 2) * 2]
            pu = self.PS[(n % 2) * 2 + 1]
            sg = self.sg[n % 2]
            a = self.aT[(gi * NT + t) % 2]
            P.act(sg[:], pg[:], AF.Silu)
            P.tt(a[:, fi, :], pu[:], sg[:], ALU.mult)
            if fi == G - 1:
                wv = self.wv[slots[tbase + gi]]
                tsl = slice(t * 512, (t + 1) * 512)
                for dc in range(8):
                    bank = self.PS[4 + dcount[0] % (3 if self.next_gcol is not None else 4)]
                    dcount[0] += 1
                    for f in range(G):
                        P.matmul(bank[:], wv["down"][:, f, dc * 128:(dc + 1) * 128], a[:, f, :], start=(f == 0), stop=(f == G - 1))
                    P.stt(self.xT[:, dc, tsl], bank[:], 0.5, self.xT[:, dc, tsl], ALU.mult, ALU.add)
                if t == NT - 1:
                    self.prefetch()
                if gi == ngroups - 1 and self.next_gcol is not None:
                    self.rmsnorm_tile(t, self.next_gcol, inter=True)
                    if t == NT - 1:
                        self.norm_done = True

        N = len(units)
        U(0)
        for n in range(N):
            if n + 1 < N:
                U(n + 1)
            E(n)


    def gla_load(self, j, h, slot):
        P = self.P
        win = self.gw_in[slot]
        wout = self.gw_out[slot]
        W = self.d["gla_w_in"][j].rearrange("(c p) n -> p c n", p=128)
        P.dma(win[:, :, 0:128], W[:, :, h * 128:(h + 1) * 128], eng="pool")
        P.dma(win[:, :, 128:256], W[:, :, 512 + h * 128:512 + (h + 1) * 128], eng="pool")
        P.dma(win[:, :, 256:512], W[:, :, 1024 + h * 256:1024 + (h + 1) * 256], eng="pool")
        P.dma(win[:, :, 512:768], W[:, :, 2048 + h * 256:2048 + (h + 1) * 256], eng="pool")
        wo = self.d["gla_w_out"][j][h * 256:(h + 1) * 256, :].rearrange("(g p) d -> p g d", p=128)
        P.dma(wout[:], wo, eng="pool")

    def gla_tasks(self, i):
        j = i // 2
        return [((lambda slot, h=h: self.gla_load(j, h, slot)), False) for h in range(4)]

    def gla(self, i, tb):
        P = self.P
        j = i // 2
        PS = self.PS
        self.rmsnorm_all((i * 3 + 1) * 8)
        W = self.d["gla_w_in"][j].rearrange("(c p) n -> p c n", p=128)
        P.memset(self.g_wlr[:], 0.0, eng="pool")
        P.dma(self.g_wlr[:, :, 0:16], W[:, :, 3072:3088], eng="pool")
        P.dma(self.g_wlr[:, :, 32:48], W[:, :, 3088:3104], eng="pool")
        P.dma(self.g_w2[0:16, :], self.d["gla_w_gate2"][j, 0], eng="pool")
        P.dma(self.g_w2[16:17, :], self.d["gla_b_gate"][j, 0:1, :], eng="pool")
        P.dma(self.g_w2[32:48, :], self.d["gla_w_gate2"][j, 1], eng="pool")
        P.dma(self.g_w2[48:49, :], self.d["gla_b_gate"][j, 1:2, :], eng="pool")
        P.memset(self.g_lrT[:], 1.0, eng="pool")
        for t in range(NT):
            tsl = slice(t * 512, (t + 1) * 512)
            ps = PS[t % 2]
            for c in range(8):
                P.matmul(ps[0:48, :], self.g_wlr[:, c, 0:48], self.hT[:, c, tsl], start=(c == 0), stop=(c == 7))
            P.copy(self.g_lrT[0:16, tsl], ps[0:16, :], eng="dve")
            P.copy(self.g_lrT[32:48, tsl], ps[32:48, :], eng="dve")
        if getattr(self, "dbg_stop", 99) <= 1:
            return
        for h in range(4):
            self.gla_head(j, h, self.task_slot[tb + h])
            self.prefetch()

    def gla_head(self, j, h, slot):
        P = self.P
        PS = self.PS
        win = self.gw_in[slot]
        wout = self.gw_out[slot]
        vtok, oT, qtl, kdc, qk = self.g_vtok, self.g_oT, self.g_qtl, self.g_kdc, self.g_qk
        T = self.tri
        for b in range(16):
            pv = PS[6 + (b // 2) % 2]
            half = b % 2
            for c in range(8):
                P.matmul(pv[:, half * 256:(half + 1) * 256], self.hT[:, c, b * 128:(b + 1) * 128], win[:, c, 256:512],
                         start=(c == 0), stop=(c == 7))
            if half == 1:
                src = pv[:].re("p (a b) -> p a b", a=2)
                P.copy(vtok[:, b - 1:b + 1, :], src, eng=("act" if (b // 2) % 2 == 0 else "dve"))
        bZ, bB, bT, bS, bO = PS[0], PS[1], PS[2], PS[3], PS[4]
        for t in range(NT):
            tsl = slice(t * 512, (t + 1) * 512)
            pq, pk = PS[5], PS[6]
            bZ2 = PS[7]
            for c in range(8):
                P.matmul(pq[:], win[:, c, 0:128], self.hT[:, c, tsl], start=(c == 0), stop=(c == 7))
            for c in range(8):
                P.matmul(pk[:], win[:, c, 128:256], self.hT[:, c, tsl], start=(c == 0), stop=(c == 7))
            P.act(qk[:, 0, :], pq[:], AF.Copy, scale=128.0 ** -0.5)
            P.copy(qk[:, 1, :], pk[:], eng="dve")
            for cl in range(4):
                c = t * 4 + cl
                csl = slice(c * 128, (c + 1) * 128)
                lsl = slice(cl * 128, (cl + 1) * 128)
                la, eq, ek, ekd = self.g_la[c % 2], self.g_eq[c % 2], self.g_ek[c % 2], self.g_ekd[c % 2]
                ktl, sm = self.g_kt[c % 2], self.g_sm[c % 2]
                P.matmul(bZ[:, 0:128], self.g_lrT[0:17, csl], self.g_w2[0:17, h * 128:(h + 1) * 128])
                P.matmul(bZ2[:, 0:128], self.g_lrT[32:49, csl], self.g_w2[32:49, h * 128:(h + 1) * 128])
                P.transpose(bT[:, 0:128], qk[:, 1, lsl], self.ident[:])
                laf = la[:].re("p a b -> p (a b)")
                P.act(la[:, 0, :], bZ[:, 0:128], AF.Exp, scale=-1.0)
                P.act(la[:, 1, :], bZ2[:, 0:128], AF.Exp, scale=-1.0)
                P.act(laf, laf, AF.Ln, bias=self.one_t[:])
                def mm_r(out, lhsT, rhs):
                    P.add("pe", lambda e: e.matmul(out.ap, lhsT.ap.bitcast(mybir.dt.float32r),
                                                   rhs.ap.bitcast(mybir.dt.float32r), start=True, stop=True),
                          reads=[lhsT, rhs], writes=[out])
                mmc = mm_r if getattr(self, "use_f32r", False) else P.matmul
                mmc(bB[:, 0:128], la[:, 0, :], T[0][:])
                mmc(bB[:, 128:256], la[:, 1, :], T[1][:])
                mmc(bB[:, 256:384], T[2][:], la[:, 0, :])
                mmc(bB[:, 384:512], T[3][:], la[:, 1, :])
                P.act(eq[:].re("p a b -> p (a b)"), bB[:, 0:256], AF.Exp, scale=-1.0 / 16)
                P.act(ek[:].re("p a b -> p (a b)"), bB[:, 0:256], AF.Exp, scale=1.0 / 16)
                P.act(ekd[:].re("p a b -> p (a b)"), bB[:, 256:512], AF.Exp, scale=-1.0 / 16)
                P.copy(self.g_eend[:, c:c + 1], eq[:, 0, 127:128], eng="dve")
                P.copy(self.g_eend[:, 16 + c:17 + c], eq[:, 1, 0:1], eng="dve")
                for dr in range(2):
                    P.tt(qtl[:, dr, csl], qk[:, 0, lsl], eq[:, dr, :], ALU.mult)
                    P.tt(ktl[:, dr, :], qk[:, 1, lsl], ek[:, dr, :], ALU.mult)
                for dr in range(2):
                    P.tt(kdc[:, c, dr, :], bT[:, 0:128], ekd[:, dr, :], ALU.mult)
                for dr in range(2):
                    P.matmul(bS[:, dr * 128:(dr + 1) * 128], ktl[:, dr, :], qtl[:, dr, csl])
                P.tt(sm[:, 0, :], bS[:, 0:128], T[0][:], ALU.mult)
                P.tt(sm[:, 1, :], bS[:, 128:256], T[2][:], ALU.mult)
                for vc in range(2):
                    for dr in range(2):
                        P.matmul(bO[:, vc * 128:(vc + 1) * 128], vtok[:, c, vc * 128:(vc + 1) * 128], sm[:, dr, :],
                                 start=(dr == 0), stop=(dr == 1))
                P.copy(oT[:, :, csl], bO[:, 0:256].re("p (a b) -> p a b", a=2), eng="act")
        for step in range(16):
            for dr in range(2):
                c = step if dr == 0 else 15 - step
                csl = slice(c * 128, (c + 1) * 128)
                bI, bK = PS[dr * 2], PS[dr * 2 + 1]
                S = self.g_S[dr]
                if step > 0:
                    Sb = self.g_Sbf[dr][(step - 1) % 2]
                    for vc in range(2):
                        P.matmul(bI[:, vc * 128:(vc + 1) * 128], Sb[:, vc * 128:(vc + 1) * 128], qtl[:, dr, csl])
                    P.tt(oT[:, :, csl], bI[:, 0:256].re("p (a b) -> p a b", a=2), oT[:, :, csl], ALU.add)
                if step < 15:
                    P.matmul(bK[:, 0:256], kdc[:, c, dr, :], vtok[:, c, :])
                    if step == 0:
                        P.copy(S[:], bK[:, 0:256], eng="dve")
                    else:
                        col = dr * 16 + c
                        P.stt(S[:], S[:], self.g_eend[:, col:col + 1], bK[:, 0:256], ALU.mult, ALU.add)
                    P.copy(self.g_Sbf[dr][step % 2][:], S[:], eng="act")
        if getattr(self, "dbg_stop", 99) <= 3:
            return
        for t in range(NT):
            tsl = slice(t * 512, (t + 1) * 512)
            ss, rs = PS[6], PS[7]
            for vc in range(2):
                sq = self.sq[vc]
                P.act(sq[:], oT[:, vc, tsl], AF.Square)
                P.matmul(ss[:], self.ones_bf[:], sq[:], start=(vc == 0), stop=(vc == 1))
            lt = self.lnt[t % 2]
            P.act(lt[:], ss[:], AF.Ln, bias=self.eps_t[:], scale=1.0 / 256)
            P.act(rs[:], lt[:], AF.Exp, scale=-0.5)
            for vc in range(2):
                pg = PS[vc]
                for c in range(8):
                    P.matmul(pg[:], win[:, c, 512 + vc * 128:512 + (vc + 1) * 128], self.hT[:, c, tsl],
                             start=(c == 0), stop=(c == 7))
                P.act(self.g_sg[:], pg[:], AF.Silu)
                P.stt(oT[:, vc, tsl], oT[:, vc, tsl], self.gno[:, j * 2 + vc:j * 2 + vc + 1], rs[:], ALU.mult, ALU.mult)
                P.tt(self.g_onb[:, vc, :], oT[:, vc, tsl], self.g_sg[:], ALU.mult)
            for dc in range(8):
                pb_ = PS[2 + dc % 4]
                for vc in range(2):
                    P.matmul(pb_[:], wout[:, vc, dc * 128:(dc + 1) * 128], self.g_onb[:, vc, :], start=(vc == 0), stop=(vc == 1))
                P.tt(self.xT[:, dc, tsl], pb_[:], self.xT[:, dc, tsl], ALU.add)
            if h == 3 and self.next_gcol is not None:
                self.rmsnorm_tile(t, self.next_gcol, inter=True)
                if t == NT - 1:
                    self.norm_done = True


    def diff_load(self, j, h, slot):
        P = self.P
        win = self.dw_in[slot]
        W = self.d["diff_w_in"][j].rearrange("(c p) n -> p c n", p=128)
        for q in range(3):
            P.dma(win[:, :, q * 128:(q + 1) * 128], W[:, :, q * 1024 + h * 128:q * 1024 + (h + 1) * 128], eng="pool")

    def diff_load_out(self, j, half, slot):
        P = self.P
        wo = self.d["diff_w_out"][j][half * 512:(half + 1) * 512, :].rearrange("(g p) d -> p g d", p=128)
        P.dma(self.dw_out[slot][:], wo, eng="pool")

    def diff_tasks(self, i):
        j = i // 2
        t = [((lambda slot, h=h: self.diff_load(j, h, slot)), False) for h in range(8)]
        t.append(((lambda slot: self.diff_load_out(j, 0, slot)), False))
        t.append(((lambda slot: self.diff_load_out(j, 1, slot)), False))
        return t

    def diff(self, i, tb):
        import math
        P = self.P
        PS = self.PS
        j = i // 2
        lam_init = 0.8 - 0.6 * math.exp(-0.3 * i)
        dstop = getattr(self, "dbg_dstop", 99)
        if dstop <= 0:
            return
        self.rmsnorm_all((i * 3 + 1) * 8)
        sm = self.d_small
        gq = self.d["diff_qk_norm_g"][j, 0].rearrange("(p o) -> p o", o=1)
        gk = self.d["diff_qk_norm_g"][j, 1].rearrange("(p o) -> p o", o=1)
        P.dma(sm[0:64, 0:1], gq, eng="sp")
        P.dma(sm[64:128, 0:1], gq, eng="sp")
        P.dma(sm[0:64, 1:2], gk, eng="sp")
        P.dma(sm[64:128, 1:2], gk, eng="sp")
        P.dma(sm[:, 2:3], self.d["diff_subln_g"][j].rearrange("(p o) -> p o", o=1), eng="sp")
        P.dma(self.d_lv[:], self.d["diff_lambda"][j].partition_broadcast(128), eng="sp")
        P.ts(sm[:, 3:4], sm[:, 0:1], 64.0 ** -0.5, None, ALU.mult)
        P.ts(sm[:, 4:5], sm[:, 2:3], 1.0 - lam_init, None, ALU.mult)
        lp = self.d_lp
        P.tt(lp[:, 0:64], self.d_lv[:, 0:64], self.d_lv[:, 64:128], ALU.mult)
        P.tt(lp[:, 64:128], self.d_lv[:, 128:192], self.d_lv[:, 192:256], ALU.mult)
        P.add("dve", lambda e: e.reduce_sum(sm[:, 5:6].ap, lp[:, 0:64].ap, mybir.AxisListType.X),
              reads=[lp[:, 0:64]], writes=[sm[:, 5:6]])
        P.add("dve", lambda e: e.reduce_sum(sm[:, 6:7].ap, lp[:, 64:128].ap, mybir.AxisListType.X),
              reads=[lp[:, 64:128]], writes=[sm[:, 6:7]])
        P.act(sm[:, 7:9], sm[:, 5:7], AF.Exp)
        P.tt(sm[:, 9:10], sm[:, 8:9], sm[:, 7:8], ALU.subtract)
        P.ts(sm[:, 10:11], sm[:, 9:10], -lam_init, None, ALU.add)
        gqs, gks, gsub, nlam = sm[:, 3:4], sm[:, 1:2], sm[:, 4:5], sm[:, 10:11]
        if dstop <= 1:
            return

        P.memset(self.d_qz[0][64:128, :], 0.0, eng="pool")
        P.memset(self.d_qz[1][0:64, :], 0.0, eng="pool")
        for h in range(8):
            win = self.dw_in[self.task_slot[tb + h]]
            for which, dst, gcol in ((0, None, 3), (1, self.d_kn, 1)):
                for t in range(NT):
                    tsl = slice(t * 512, (t + 1) * 512)
                    pq = PS[(2 * t) % 4 + 4 * which]
                    ss = PS[(2 * t) % 4 + 1 + 4 * which]
                    for c in range(8):
                        P.matmul(pq[:], win[:, c, which * 128:(which + 1) * 128], self.hT[:, c, tsl],
                                 start=(c == 0), stop=(c == 7))
                    sq = self.sq[t % 2]
                    P.act(sq[:], pq[:], AF.Square)
                    P.matmul(ss[:], self.blk2[:], sq[:])
                    lt = self.lnt[t % 2]
                    P.act(lt[:], ss[:], AF.Ln, bias=self.eps_t[:], scale=1.0 / 64)
                    P.act(self.d_rs[:], lt[:], AF.Exp, scale=-0.5)
                    if which == 0:
                        P.stt(self.d_qz[0][0:64, tsl], pq[0:64, :], sm[0:64, gcol:gcol + 1], self.d_rs[0:64, :], ALU.mult, ALU.mult)
                        P.stt(self.d_qz[1][64:128, tsl], pq[64:128, :], sm[64:128, gcol:gcol + 1], self.d_rs[64:128, :], ALU.mult, ALU.mult)
                    else:
                        P.stt(dst[:, tsl], pq[:], sm[:, gcol:gcol + 1], self.d_rs[:], ALU.mult, ALU.mult)
            if dstop <= 2:
                return
            for b in range(16):
                pv = PS[(b // 4) % 2]
                q4 = b % 4
                for c in range(8):
                    P.matmul(pv[:, q4 * 128:(q4 + 1) * 128], self.hT[:, c, b * 128:(b + 1) * 128], win[:, c, 256:384],
                             start=(c == 0), stop=(c == 7))
                if q4 == 3:
                    src = pv[:].re("p (a b) -> p a b", a=4)
                    P.copy(self.d_vtok[:, b - 3:b + 1, :], src, eng=("act" if (b // 4) % 2 == 0 else "dve"))
            if dstop <= 3:
                return
            src = bass.AP(self.ebscr, h * 128 * 1279 + 127, [[1278, 128], [1, 1152]])
            op = P.dma(self.d_eb[:], src, eng="sp")
            for w in self.eb_writes:
                if w not in op.deps:
                    op.deps.append(w)
                    w.needs_sig = True
            if dstop <= 4:
                P.copy(self.xT[:, 0, :], self.d_qz[0][:], eng="dve")
                P.copy(self.xT[:, 1, :], self.d_kn[:], eng="dve")
                P.copy(self.xT[:, 2, 0:1152], self.d_eb[:], eng="dve")
                P.copy(self.xT[:, 3, 0:16], sm[:], eng="dve")
                P.copy(self.xT[:, 3, 16:32], self.cfar[:], eng="dve")
                P.copy(self.xT[:, 4, :], self.d_vtok[:].re("p a b -> p (a b)"), eng="dve")
                return
            pend = None
            for qt in range(NT):
                pend = self.diff_attn(h, qt, nlam, gsub, pend)
                if qt == NT - 1:
                    pend()
                if dstop <= 5:
                    P.copy(self.xT[:, 0, 0:512], self.d_on[:, 0, 0:512], eng="dve")
                    P.copy(self.xT[:, 1, 0:512], self.d_t[2][:], eng="dve")
                    P.copy(self.xT[:, 2, 0:512], self.d_rz[0][:], eng="dve")
                    P.copy(self.xT[:, 3, 0:512], self.d_rz[1][:], eng="dve")
                    P.copy(self.xT[:, 4, 0:512], self.d_t[0][:], eng="dve")
                    P.copy(self.xT[:, 5, 0:512], self.d_pt[3][:], eng="dve")
                    P.copy(self.xT[:, 6, 0:512], self.d_pt[0][:], eng="dve")
                    return
            self.prefetch()
        wouts = [self.dw_out[self.task_slot[tb + 8]], self.dw_out[self.task_slot[tb + 9]]]
        for t in range(NT):
            tsl = slice(t * 512, (t + 1) * 512)
            for dc in range(8):
                pb = PS[(dc + 8 * t) % 7] if self.next_gcol is not None else PS[dc % 4 + 4 * (t % 2)]
                for hh in range(8):
                    P.matmul(pb[:], wouts[hh // 4][:, hh % 4, dc * 128:(dc + 1) * 128], self.d_on[:, hh, tsl],
                             start=(hh == 0), stop=(hh == 7))
                P.tt(self.xT[:, dc, tsl], pb[:], self.xT[:, dc, tsl], ALU.add)
            if self.next_gcol is not None:
                self.rmsnorm_tile(t, self.next_gcol, inter=True)
                if t == NT - 1:
                    self.norm_done = True
        self.prefetch()
        self.prefetch()

    def diff_attn(self, h, qt, nlam, gsub, pending):
        P = self.P
        PS = self.PS
        qsl = slice(qt * 512, (qt + 1) * 512)
        kn, vt = self.d_kn, self.d_vtok
        O = [PS[4], PS[5]]
        Z = [PS[6], PS[7]]

        def S(kc):
            ksl = slice(kc * 128, (kc + 1) * 128)
            for comp in range(2):
                P.matmul(PS[(kc % 2) * 2 + comp][:], kn[:, ksl], self.d_qz[comp][:, qsl])

        S(0)
        for kc in range(16):
            if kc == 3 and pending is not None:
                pending()
                pending = None
            if kc + 1 < 16:
                S(kc + 1)
            delta = kc - 4 * qt
            for comp in range(2):
                sb = PS[(kc % 2) * 2 + comp]
                pt = self.d_pt[(kc % 2) * 2 + comp]
                if -1 <= delta <= 4:
                    e = self.d_e[(2 * kc + comp) % 3]
                    P.act(e[:], sb[:], AF.Exp)
                    off = 512 - 128 * delta
                    P.tt(pt[:], e[:], self.d_eb[:, off:off + 512], ALU.mult)
                else:
                    sgn = 0 if delta < 0 else 1
                    P.act(pt[:], sb[:], AF.Exp, bias=self.cfar[:, 2 * h + sgn:2 * h + sgn + 1])
            for comp in range(2):
                pt = self.d_pt[(kc % 2) * 2 + comp]
                P.matmul(O[comp][:], vt[:, kc, :], pt[:], start=(kc == 0), stop=(kc == 15))
                P.matmul(Z[comp][:], self.ones_bf[:], pt[:], start=(kc == 0), stop=(kc == 15))
        o1c, o2c = self.d_t[0], self.d_t[1]
        for comp in range(2):
            lt = self.lnt[comp]
            P.act(lt[:], Z[comp][:], AF.Ln)
            P.act(self.d_rz[comp][:], lt[:], AF.Exp, scale=-1.0)
        P.copy(o1c[:], O[0][:], eng="dve")
        P.copy(o2c[:], O[1][:], eng="dve")

        def epilogue():
            P.tt(o1c[:], o1c[:], self.d_rz[0][:], ALU.mult)
            P.stt(o2c[:], o2c[:], nlam, self.d_rz[1][:], ALU.mult, ALU.mult)
            P.tt(o1c[:], o1c[:], o2c[:], ALU.add)
            sq = self.sq[0]
            P.act(sq[:], o1c[:], AF.Square)
            ss = PS[0]
            P.matmul(ss[:], self.ones_bf[:], sq[:])
            lt = self.d_t[2]
            P.act(lt[:], ss[:], AF.Ln, bias=self.eps_t[:], scale=1.0 / 128)
            P.act(self.d_rs[:], lt[:], AF.Exp, scale=-0.5)
            P.stt(self.d_on[:, h, qsl], o1c[:], gsub, self.d_rs[:], ALU.mult, ALU.mult)
        return epilogue

    def build(self):
        P = self.P
        P.memset(self.eps_t[:], EPS, eng="pool")
        P.memset(self.one_t[:], 1.0, eng="pool")
        self.eb_writes = []
        self.setup()
        if any(st[0] == "diff" for st in self.plan):
            self.setup_bias()
        bases = []
        for s in range(self.n_seq):
            for step in self.plan:
                bases.append(len(self.load_q))
                if step[0] == "ffn":
                    self.load_q += self.ffn_tasks(step[1], step[2])
                elif step[0] == "gla":
                    self.load_q += self.gla_tasks(step[1])
                elif step[0] == "diff":
                    self.load_q += self.diff_tasks(step[1])
                elif step[0] == "dbg_gate":
                    self.load_q += self.ffn_tasks(0, 0)[:1]
        self.pump()
        k = 0
        for s in range(self.n_seq):
            self.load_x(s)
            for si, step in enumerate(self.plan):
                tb = bases[k]
                k += 1
                self.next_gcol = None
                if si + 1 < len(self.plan) and getattr(self, "interleave_norm", False):
                    nx = self.plan[si + 1]
                    if nx[0] == "ffn":
                        self.next_gcol = (nx[1] * 3 + (0 if nx[2] == 0 else 2)) * 8
                    elif nx[0] in ("gla", "diff"):
                        self.next_gcol = (nx[1] * 3 + 1) * 8
                if step[0] == "ffn":
                    self.ffn(step[1], step[2], tb)
                elif step[0] == "gla":
                    self.in_mixer = True
                    self.gla(step[1], tb)
                    self.in_mixer = False
                    self.pump()
                elif step[0] == "diff":
                    self.in_mixer = True
                    self.diff(step[1], tb)
                    self.in_mixer = False
                    self.pump()
                elif step[0] == "dbg_gate":
                    self.rmsnorm_all(0)
                    wv = self.wv[self.task_slot[tb]]
                    tsl = slice(0, 512)
                    pg, pu = self.PS[0], self.PS[1]
                    for c in range(8):
                        P.matmul(pg[:], wv["gate"][:, c, 0:128], self.hT[:, c, tsl], start=(c == 0), stop=(c == 7))
                    for c in range(8):
                        P.matmul(pu[:], wv["up"][:, c, 0:128], self.hT[:, c, tsl], start=(c == 0), stop=(c == 7))
                    P.copy(self.xT[:, 0, tsl], pg[:], eng="dve")
                    P.copy(self.xT[:, 2, tsl], pu[:], eng="dve")
                    P.act(self.sg[0][:], pg[:], AF.Silu)
                    P.copy(self.xT[:, 1, tsl], self.sg[0][:], eng="dve")
                    P.tt(self.aT[0][:, 0, :], pu[:], self.sg[0][:], ALU.mult)
                    P.copy(self.xT[:, 3, tsl], self.aT[0][:, 0, :], eng="dve")
                    P.copy(self.xT[:, 4, tsl], wv["gate"][:, 0, 0:512], eng="dve")
                    P.copy(self.xT[:, 5, tsl], wv["down"][:, 0, 0:512], eng="dve")
                elif step[0] == "dbg_norm":
                    self.rmsnorm_all(step[1])
                    for c in range(8):
                        P.copy(self.xT[:, c, :], self.hT[:, c, :], eng="dve")
            self.store_x(s)
        return P.emit()


FULL_PLAN = []
for _i in range(DEPTH):
    FULL_PLAN += [("ffn", _i, 0), ("gla" if _i % 2 == 0 else "diff", _i), ("ffn", _i, 1)]

_CONSTS = None


def _t5_onehot():
    import math
    import jax
    import jax.numpy as jnp
    cpu = jax.devices("cpu")[0]
    with jax.default_device(cpu):
        u = jnp.arange(1279, dtype=jnp.int32)
        rel = 639 - u
        nb = 16
        max_exact = 8
        ret = (rel > 0).astype(jnp.int32) * nb
        n = jnp.abs(rel)
        nf = jnp.maximum(n, 1).astype(jnp.float32)
        large = max_exact + (jnp.log(nf / max_exact) / math.log(128 / max_exact) * (nb - max_exact)).astype(jnp.int32)
        large = jnp.minimum(large, nb - 1)
        bucket = np.asarray(ret + jnp.where(n < max_exact, n, large))
    oh = np.zeros((32, 1281), np.float32)
    oh[bucket, np.arange(1279)] = 1.0
    oh[15, 1279] = 1.0
    oh[31, 1280] = 1.0
    return oh


def _consts():
    global _CONSTS
    if _CONSTS is None:
        c = {}
        c["c_ident"] = np.eye(128, dtype=np.float32)
        r = np.arange(128)[:, None]
        q = np.arange(128)[None, :]
        c["c_tri"] = np.stack([(r <= q), (r >= q), (r > q), (r < q)]).astype(np.float32)
        c["c_oh"] = _t5_onehot()
        _CONSTS = c
    return _CONSTS


def run_plan(inputs, plan, n_seq=2, n_cores=8, trace=False, decl_depth=DEPTH, **dbg):
    nc = bass.Bass("TRN2", target_bir_lowering=False)
    k = K(nc, n_seq, plan, decl_depth, **dbg)
    stats = k.build()
    print("program stats:", stats, flush=True)
    x = np.ascontiguousarray(np.asarray(inputs["x"], dtype=np.float32))
    shared = {}
    for name in k.d:
        if name == "x" or name.startswith("c_"):
            continue
        a = np.ascontiguousarray(np.asarray(inputs[name], dtype=np.float32))
        shp = tuple(k.d[name].shape)
        if a.size != int(np.prod(shp)):
            a = a[:shp[0]]
        shared[name] = np.ascontiguousarray(a).reshape(shp)
    shared.update({n: v for n, v in _consts().items() if n in k.d})
    in_maps = []
    for c in range(n_cores):
        m = dict(shared)
        m["x"] = x[c * n_seq:(c + 1) * n_seq]
        in_maps.append(m)
    res = run_bass_kernel_spmd(nc, in_maps, core_ids=list(range(n_cores)), trace=trace)
    out = np.concatenate([r["y"] for r in res.results], axis=0)
    return out, res


def kernel(**inputs):
    out, _ = run_plan(inputs, FULL_PLAN, n_seq=2, n_cores=8)
    return out.astype(np.float32)
```
